# Optimizing a Trainium2 kernel written in Bass

```python
import jax, jax.numpy as jnp
from jax import lax
import numpy as np

D_MODEL = 1024
BATCH = 16
SEQ = 4096
DEPTH = 1
DEC_BATCH = 32
DEC_SEQ = 16
PAST_LEN = 1024

CHUNK = 64
N_HEADS = 16
N_KV_HEADS = 4
HEAD_DIM = 64
Q_GROUP = N_HEADS // N_KV_HEADS
ATT_WIDTH = N_HEADS * HEAD_DIM
KV_WIDTH = N_KV_HEADS * HEAD_DIM
WINDOW = 128
WIN_CHUNKS = WINDOW // CHUNK
D_RNN = D_MODEL
N_LRU_BLOCKS = 8
LRU_BLOCK = D_RNN // N_LRU_BLOCKS
CONV_WIDTH = 4
LRU_C = 8.0
D_FF = 2816
EPS = 1e-6
NEG_INF = -1e30
ATT_SCALE = HEAD_DIM ** -0.5
SPLIT_POINTS = (D_RNN, 2 * D_RNN, 2 * D_RNN + ATT_WIDTH, 2 * D_RNN + ATT_WIDTH + KV_WIDTH,
                2 * D_RNN + ATT_WIDTH + 2 * KV_WIDTH, 2 * D_RNN + ATT_WIDTH + 2 * KV_WIDTH + D_MODEL)
D_IN = 2 * D_RNN + ATT_WIDTH + 2 * KV_WIDTH + 2 * D_MODEL

kernel_name = 'hawk_swa_sink_macaron_stream_step'


def rms_norm(x, g):
    x32 = x.astype(jnp.float32)
    y = x32 * lax.rsqrt(jnp.mean(x32 * x32, axis=-1, keepdims=True) + EPS)
    return (y * g.astype(jnp.float32)).astype(x.dtype)


def swiglu(x, w_gate, w_up, w_down):
    return (jax.nn.silu(x @ w_gate) * (x @ w_up)) @ w_down


def causal_conv(xin, prev, w, b):
    T = xin.shape[1]
    xp = jnp.concatenate([prev.astype(xin.dtype), xin], axis=1)
    out = b + sum(xp[:, j:j + T] * w[j] for j in range(CONV_WIDTH))
    return out, xp[:, -(CONV_WIDTH - 1):]


def rg_lru(xc, h0, w_rg, b_rg, w_ig, b_ig, lam):
    B, T, _ = xc.shape
    x32 = xc.astype(jnp.float32)
    xb = x32.reshape(B, T, N_LRU_BLOCKS, LRU_BLOCK)
    r = jax.nn.sigmoid(jnp.einsum('btnd,nde->btne', xb, w_rg.astype(jnp.float32)) + b_rg.astype(jnp.float32))
    i = jax.nn.sigmoid(jnp.einsum('btnd,nde->btne', xb, w_ig.astype(jnp.float32)) + b_ig.astype(jnp.float32))
    r = r.reshape(B, T, D_RNN)
    i = i.reshape(B, T, D_RNN)
    log_a = -LRU_C * r * jax.nn.softplus(-lam.astype(jnp.float32))
    a = jnp.exp(log_a)
    bterm = jnp.sqrt(-jnp.expm1(2.0 * log_a)) * (i * x32)
    bterm = bterm.at[:, 0].add(a[:, 0] * h0.astype(jnp.float32))

    def combine(left, right):
        a1, b1 = left
        a2, b2 = right
        return a1 * a2, a2 * b1 + b2

    _, h = lax.associative_scan(combine, (a, bterm), axis=1)
    return h.astype(xc.dtype), h[:, -1].astype(h0.dtype)


def sink_probs(scores, valid, sinks):
    s = jnp.where(valid, scores, NEG_INF)
    sink = sinks.astype(jnp.float32).reshape(N_KV_HEADS, Q_GROUP, 1, 1)
    m = jnp.maximum(jnp.max(s, axis=-1, keepdims=True), sink)
    p = jnp.exp(s - m)
    return p / (jnp.sum(p, axis=-1, keepdims=True) + jnp.exp(sink - m))


def banded_window_attention(q, k, v, sinks):
    B, T = q.shape[:2]
    nc = T // CHUNK
    qc = q.reshape(B, nc, CHUNK, N_KV_HEADS, Q_GROUP, HEAD_DIM)
    pad = ((0, 0), (WIN_CHUNKS * CHUNK, 0), (0, 0), (0, 0))
    kp = jnp.pad(k, pad).reshape(B, nc + WIN_CHUNKS, CHUNK, N_KV_HEADS, HEAD_DIM)
    vp = jnp.pad(v, pad).reshape(B, nc + WIN_CHUNKS, CHUNK, N_KV_HEADS, HEAD_DIM)
    kb = jnp.concatenate([kp[:, j:j + nc] for j in range(WIN_CHUNKS + 1)], axis=2)
    vb = jnp.concatenate([vp[:, j:j + nc] for j in range(WIN_CHUNKS + 1)], axis=2)
    key_chunk = (jnp.arange(nc)[:, None] - WIN_CHUNKS
                 + jnp.repeat(jnp.arange(WIN_CHUNKS + 1), CHUNK)[None, :])
    valid = (key_chunk >= 0)[:, None, None, None, :]
    scores = jnp.einsum('bcqkgd,bcskd->bckgqs', qc, kb, preferred_element_type=jnp.float32) * ATT_SCALE
    p = sink_probs(scores, valid, sinks)
    out = jnp.einsum('bckgqs,bcskd->bcqkgd', p.astype(v.dtype), vb)
    return out.reshape(B, T, ATT_WIDTH)


def cached_window_attention(q, k, v, cache_k, cache_v, sinks):
    B, S = q.shape[:2]
    cw = cache_k.shape[1]
    kk = jnp.concatenate([cache_k.astype(k.dtype), k], axis=1)
    vv = jnp.concatenate([cache_v.astype(v.dtype), v], axis=1)
    q_pos = PAST_LEN + jnp.arange(S)
    k_pos = jnp.concatenate([PAST_LEN - cw + jnp.arange(cw), q_pos])
    qch = (q_pos // CHUNK)[:, None]
    kch = (k_pos // CHUNK)[None, :]
    valid = (kch <= qch) & (qch - kch <= WIN_CHUNKS) & (k_pos[None, :] >= 0)
    qg = q.reshape(B, S, N_KV_HEADS, Q_GROUP, HEAD_DIM)
    scores = jnp.einsum('bqkgd,bskd->bkgqs', qg, kk, preferred_element_type=jnp.float32) * ATT_SCALE
    p = sink_probs(scores, valid, sinks)
    out = jnp.einsum('bkgqs,bskd->bqkgd', p.astype(vv.dtype), vv)
    return out.reshape(B, S, ATT_WIDTH)


def layer(x, conv_prev, h0, attend, p):
    B, T = x.shape[:2]
    h = x + 0.5 * swiglu(rms_norm(x, p['norm_ff1']), p['ff1_gate'], p['ff1_up'], p['ff1_down'])
    u = rms_norm(h, p['norm_mix'])
    x_rnn, g_rnn, q, k, v, gate_r, gate_a = jnp.split(u @ p['w_in'], SPLIT_POINTS, axis=-1)
    conv_out, conv_state = causal_conv(x_rnn, conv_prev, p['conv_w'], p['conv_b'])
    lru_out, h_last = rg_lru(conv_out, h0, p['w_rg'], p['b_rg'], p['w_ig'], p['b_ig'], p['lru_lambda'])
    rec = jax.nn.gelu(g_rnn) * lru_out
    q = q.reshape(B, T, N_HEADS, HEAD_DIM)
    k = k.reshape(B, T, N_KV_HEADS, HEAD_DIM)
    v = v.reshape(B, T, N_KV_HEADS, HEAD_DIM)
    att = attend(q, k, v, p['attn_sinks'])
    branch_r = rec @ p['w_branch'][:D_RNN]
    branch_a = att @ p['w_branch'][D_RNN:]
    merged = jax.nn.sigmoid(gate_r) * branch_r + jax.nn.sigmoid(gate_a) * branch_a
    h = h + merged @ p['w_out']
    h = h + 0.5 * swiglu(rms_norm(h, p['norm_ff2']), p['ff2_gate'], p['ff2_up'], p['ff2_down'])
    return h, k, v, conv_state, h_last


def setup_inputs(seed: int = 0) -> dict:
    key = jax.random.key(seed)
    ks = jax.random.split(key, 32)
    f32 = jnp.float32
    cw = min(WINDOW, PAST_LEN)

    def nrm(k, shape, scale):
        return jax.random.normal(k, shape, f32) * scale

    a0 = jax.random.uniform(ks[20], (DEPTH, D_RNN), f32, minval=0.9, maxval=0.999)
    return {
        'x_prompt': nrm(ks[0], (BATCH, SEQ, D_MODEL), 1.0),
        'x_sample': nrm(ks[1], (DEC_BATCH, DEC_SEQ, D_MODEL), 1.0),
        'cache_k': nrm(ks[2], (DEPTH, DEC_BATCH, cw, N_KV_HEADS, HEAD_DIM), 1.0),
        'cache_v': nrm(ks[3], (DEPTH, DEC_BATCH, cw, N_KV_HEADS, HEAD_DIM), 1.0),
        'state_conv': nrm(ks[4], (DEPTH, DEC_BATCH, CONV_WIDTH - 1, D_RNN), 1.0),
        'state_lru': nrm(ks[5], (DEPTH, DEC_BATCH, D_RNN), 0.5),
        'norm_ff1': 1.0 + nrm(ks[6], (DEPTH, D_MODEL), 0.01),
        'ff1_gate': nrm(ks[7], (DEPTH, D_MODEL, D_FF), D_MODEL ** -0.5),
        'ff1_up': nrm(ks[8], (DEPTH, D_MODEL, D_FF), D_MODEL ** -0.5),
        'ff1_down': nrm(ks[9], (DEPTH, D_FF, D_MODEL), D_FF ** -0.5),
        'norm_mix': 1.0 + nrm(ks[10], (DEPTH, D_MODEL), 0.01),
        'w_in': nrm(ks[11], (DEPTH, D_MODEL, D_IN), D_MODEL ** -0.5),
        'conv_w': nrm(ks[12], (DEPTH, CONV_WIDTH, D_RNN), CONV_WIDTH ** -0.5),
        'conv_b': nrm(ks[13], (DEPTH, D_RNN), 0.01),
        'w_rg': nrm(ks[14], (DEPTH, N_LRU_BLOCKS, LRU_BLOCK, LRU_BLOCK), LRU_BLOCK ** -0.5),
        'b_rg': nrm(ks[15], (DEPTH, N_LRU_BLOCKS, LRU_BLOCK), 0.01),
        'w_ig': nrm(ks[16], (DEPTH, N_LRU_BLOCKS, LRU_BLOCK, LRU_BLOCK), LRU_BLOCK ** -0.5),
        'b_ig': nrm(ks[17], (DEPTH, N_LRU_BLOCKS, LRU_BLOCK), 0.01),
        'lru_lambda': jnp.log(a0) - jnp.log1p(-a0),
        'attn_sinks': nrm(ks[18], (DEPTH, N_HEADS), 0.5),
        'w_branch': nrm(ks[19], (DEPTH, D_RNN + ATT_WIDTH, D_MODEL), D_RNN ** -0.5),
        'w_out': nrm(ks[21], (DEPTH, D_MODEL, D_MODEL), D_MODEL ** -0.5),
        'norm_ff2': 1.0 + nrm(ks[22], (DEPTH, D_MODEL), 0.01),
        'ff2_gate': nrm(ks[23], (DEPTH, D_MODEL, D_FF), D_MODEL ** -0.5),
        'ff2_up': nrm(ks[24], (DEPTH, D_MODEL, D_FF), D_MODEL ** -0.5),
        'ff2_down': nrm(ks[25], (DEPTH, D_FF, D_MODEL), D_FF ** -0.5),
        'norm_final': 1.0 + nrm(ks[26], (D_MODEL,), 0.01),
    }


def reference(x_prompt, x_sample, cache_k, cache_v, state_conv, state_lru,
              norm_ff1, ff1_gate, ff1_up, ff1_down, norm_mix, w_in, conv_w, conv_b,
              w_rg, b_rg, w_ig, b_ig, lru_lambda, attn_sinks, w_branch, w_out,
              norm_ff2, ff2_gate, ff2_up, ff2_down, norm_final):
    hp, hs = x_prompt, x_sample
    kp_l, vp_l, cp_l, lp_l, ks_l, vs_l, cs_l, ls_l = [], [], [], [], [], [], [], []
    for l in range(DEPTH):
        p = {'norm_ff1': norm_ff1[l], 'ff1_gate': ff1_gate[l], 'ff1_up': ff1_up[l], 'ff1_down': ff1_down[l],
             'norm_mix': norm_mix[l], 'w_in': w_in[l], 'conv_w': conv_w[l], 'conv_b': conv_b[l],
             'w_rg': w_rg[l], 'b_rg': b_rg[l], 'w_ig': w_ig[l], 'b_ig': b_ig[l],
             'lru_lambda': lru_lambda[l], 'attn_sinks': attn_sinks[l], 'w_branch': w_branch[l],
             'w_out': w_out[l], 'norm_ff2': norm_ff2[l], 'ff2_gate': ff2_gate[l], 'ff2_up': ff2_up[l],
             'ff2_down': ff2_down[l]}
        conv0 = jnp.zeros((hp.shape[0], CONV_WIDTH - 1, D_RNN), hp.dtype)
        h0 = jnp.zeros((hp.shape[0], D_RNN), state_lru.dtype)
        hp, k_p, v_p, conv_p, lru_p = layer(hp, conv0, h0, banded_window_attention, p)
        kp_l.append(k_p[:, -WINDOW:])
        vp_l.append(v_p[:, -WINDOW:])
        cp_l.append(conv_p)
        lp_l.append(lru_p)
        ck, cv = cache_k[l], cache_v[l]
        attend_s = lambda q, k, v, s, ck=ck, cv=cv: cached_window_attention(q, k, v, ck, cv, s)
        hs, k_s, v_s, conv_s, lru_s = layer(hs, state_conv[l], state_lru[l], attend_s, p)
        ks_l.append(k_s)
        vs_l.append(v_s)
        cs_l.append(conv_s)
        ls_l.append(lru_s)
    y_prompt = rms_norm(hp, norm_final)
    y_sample = rms_norm(hs, norm_final)
    return (y_prompt, y_sample,
            jnp.stack(kp_l), jnp.stack(vp_l), jnp.stack(cp_l), jnp.stack(lp_l),
            jnp.stack(ks_l), jnp.stack(vs_l), jnp.stack(cs_l), jnp.stack(ls_l))
```

```python
import os
import numpy as np
from contextlib import ExitStack
import concourse.bass as bass
import concourse.mybir as mybir
from concourse.bass_utils import run_bass_kernel_spmd

F32 = mybir.dt.float32
BF16 = mybir.dt.bfloat16
AF = mybir.ActivationFunctionType
ALU = mybir.AluOpType

NCORES = 8
D = 1024
FF = 2816
FC = 22
TP = 512
TS = 64
EPS = 1e-6
NRING = 6
SLOT = 2048
NTMP = 12
GU, DH, WI = 2048, 1408, 2048

ENG_SEM = {'pe': 'S_pe', 'act': 'S_act', 'dve': 'S_dve', 'pool': 'S_pool'}


class Planner:
    def __init__(self, same_sync=True):
        self.prog = {e: [] for e in ('pe', 'act', 'dve', 'pool', 'sp')}
        self.cnt = {}
        self.lastw = {}
        self.readers = {}
        self.waited = {e: {} for e in self.prog}
        self.same_sync = same_sync
        self.sems = {}
        self.semnames = set(ENG_SEM.values())

    def _collect(self, reads, writes):
        deps = {}

        def add(d):
            if d is None:
                return
            s, v = d
            if deps.get(s, 0) < v:
                deps[s] = v
        for k in reads:
            add(self.lastw.get(k))
        for k in writes:
            add(self.lastw.get(k))
            for s, v in self.readers.get(k, {}).items():
                add((s, v))
        return deps

    def _waits(self, eng, deps, force_own=False):
        own = ENG_SEM.get(eng)
        out = []
        for s, v in deps.items():
            if s == own and not force_own and (eng == 'pe' or not self.same_sync):
                continue
            if self.waited[eng].get(s, 0) >= v:
                continue
            self.waited[eng][s] = v
            out.append((s, v))
        return out

    def _record(self, reads, writes, sem, val):
        for k in reads:
            r = self.readers.setdefault(k, {})
            if r.get(sem, 0) < val:
                r[sem] = val
        for k in writes:
            self.lastw[k] = (sem, val)
            self.readers[k] = {}

    def op(self, eng, fn, reads=(), writes=()):
        self.group(eng, [fn], reads, writes)

    def group(self, eng, fns, reads=(), writes=()):
        sem = ENG_SEM[eng]
        waits = self._waits(eng, self._collect(reads, writes))
        val = self.cnt.get(sem, 0) + 1
        self.cnt[sem] = val
        sems = self.sems

        def emit(h, waits=waits, fns=fns, sem=sem):
            for s, v in waits:
                h.wait_ge(sems[s], v)
            for f in fns[:-1]:
                f(h)
            fns[-1](h).then_inc(sems[sem], 1)
        self.prog[eng].append(emit)
        self._record(reads, writes, sem, val)

    def dma(self, q, sem, fn, reads=(), writes=()):
        self.dma_batch(q, sem, [(fn, reads, writes)])

    def dma_batch(self, q, sem, items):
        self.semnames.add(sem)
        base = self.cnt.get(sem, 0)
        total = base + 16 * len(items)
        self.cnt[sem] = total
        sems = self.sems
        for fn, reads, writes in items:
            waits = self._waits(q, self._collect(reads, writes))

            def emit(h, waits=waits, fn=fn, sem=sem):
                for s, v in waits:
                    h.wait_ge(sems[s], v)
                fn(h).then_inc(sems[sem], 16)
            self.prog[q].append(emit)
        for fn, reads, writes in items:
            self._record(reads, writes, sem, total)

    def barrier(self):
        snap = dict(self.cnt)
        for e in self.prog:
            waits = self._waits(e, snap, force_own=(e != 'pe'))
            sems = self.sems

            def emit(h, waits=waits):
                for s, v in waits:
                    h.wait_ge(sems[s], v)
            self.prog[e].append(emit)
        self.lastw.clear()
        self.readers.clear()

    def final_wait(self, eng):
        snap = dict(self.cnt)
        waits = self._waits(eng, snap, force_own=False)
        sems = self.sems

        def emit(h, waits=waits):
            for s, v in waits:
                h.wait_ge(sems[s], v)
        self.prog[eng].append(emit)


def MM(out, lhsT, rhs, start, stop):
    return lambda h: h.matmul(out, lhsT=lhsT, rhs=rhs, start=start, stop=stop)


def TR(out, in_, ident):
    return lambda h: h.transpose(out, in_, ident)


def ACT(out, in_, func, bias=None, scale=None):
    kw = {}
    if bias is not None:
        kw['bias'] = bias
    if scale is not None:
        kw['scale'] = scale
    return lambda h: h.activation(out=out, in_=in_, func=func, **kw)


def TT(out, a, b, op):
    return lambda h: h.tensor_tensor(out=out, in0=a, in1=b, op=op)


def TS1(out, a, s1, op0):
    return lambda h: h.tensor_scalar(out=out, in0=a, scalar1=s1, scalar2=None, op0=op0)


def STT(out, in0, scalar, in1, op0, op1):
    return lambda h: h.scalar_tensor_tensor(out=out, in0=in0, scalar=scalar, in1=in1, op0=op0, op1=op1)


def CP(out, in_):
    return lambda h: h.tensor_copy(out=out, in_=in_)


def RECIP(out, in_):
    return lambda h: h.reciprocal(out=out, in_=in_)


def SCAN(out, d0, d1, init):
    return lambda h: h.tensor_tensor_scan(out=out, data0=d0, data1=d1, initial=init, op0=ALU.mult, op1=ALU.add)


def MSET(out, v):
    return lambda h: h.memset(out, v)


def DMA(out, in_):
    return lambda h: h.dma_start(out=out, in_=in_)


def COPYENG(eng, out, in_):
    if eng == 'act':
        return ACT(out, in_, AF.Copy)
    return CP(out, in_)


def slab_sizes():
    seq = [GU] * FC + [DH] * 16 + [WI] * 22 + [WI] * 8 + [WI] * 4 + [GU] * FC + [DH] * 16
    offs = np.concatenate([[0], np.cumsum(seq)]).astype(np.int64)
    return seq, offs


VC_G1, VC_GM, VC_G2, VC_CB, VC_BRG, VC_BIG, VC_LAM, VC_CW = 0, 8, 16, 24, 32, 40, 48, 56
VC_GF = 88
NVEC = 96


def build_program(ntile_seq):
    nc = bass.Bass("TRN2", target_bir_lowering=False)
    P = Planner(same_sync=True)
    NSEQ = 2
    LP = ntile_seq * TP

    def din(name, shape, dt=F32):
        return nc.dram_tensor(name, list(shape), dt, kind="ExternalInput").ap()

    def dout(name, shape, dt=F32):
        return nc.dram_tensor(name, list(shape), dt, kind="ExternalOutput").ap()

    xp = din("xp", [NSEQ * LP, D])
    xs = din("xs", [TS, D])
    ckd = din("ck", [4, 128, 256])
    cvd = din("cv", [4, 128, 256])
    scvd = din("scv", [128, 8 * 4 * 4])
    hstd = din("hst", [128, 32])
    vecd = din("vecs", [128, NVEC])
    gfd = din("gf", [D])
    skd = din("sinks", [16])
    wrgd = din("wrg", [128, 1024])
    wigd = din("wig", [128, 1024])
    w_g = [din("ff1_gate", [D, FF]), din("ff2_gate", [D, FF])]
    w_u = [din("ff1_up", [D, FF]), din("ff2_up", [D, FF])]
    w_d = [din("ff1_down", [FF, D]), din("ff2_down", [FF, D])]
    w_in = din("w_in", [D, 5632])
    w_br = din("w_branch", [2048, D])
    w_o = din("w_out", [D, D])

    yp = dout("yp", [NSEQ * LP, D])
    ys = dout("ys", [TS, D])
    kpo = dout("kp", [NSEQ, 128, 256])
    vpo = dout("vp", [NSEQ, 128, 256])
    cpo = dout("cp", [NSEQ, 3, D])
    lpo = dout("lp", [NSEQ, D])
    kso = dout("ks", [TS, 256])
    vso = dout("vs", [TS, 256])
    cso = dout("cs", [4, 3, D])
    lso = dout("ls", [4, D])

    seq_sizes, seq_offs = slab_sizes()
    NSLAB = len(seq_sizes)
    TOT = int(seq_offs[-1]) + 8192
    wscr = nc.dram_tensor("wscr", [128, TOT], BF16, kind="Internal").ap()
    B_GU1, B_D1, B_WI, B_BR, B_WO, B_GU2, B_D2 = 0, 22, 38, 60, 68, 72, 94

    with ExitStack() as es:
        def sb(name, shape, dt):
            return es.enter_context(nc.sbuf_tensor(name, list(shape), dt))

        ident = sb("ident", [128, 128], F32)
        ones_bf = sb("ones_bf", [128, 128], BF16)
        vecs = sb("vecs_sb", [128, NVEC], F32)
        dv = sb("dv", [128, 40], F32)
        gtile = sb("gtile", [128, D], F32)
        sk = sb("sk", [128, 16], F32)
        sinkt = sb("sinkt", [128, 2, 512], F32)
        Dg = sb("Dg", [128, 4, 8, 128], BF16)
        wrg = sb("wrg_sb", [128, 8, 128], BF16)
        wig = sb("wig_sb", [128, 8, 128], BF16)
        hcar = sb("hcar", [128, 8], F32)
        scv = sb("scv_sb", [128, 8, 4, 4], F32)
        hst = sb("hst_sb", [128, 8, 4], F32)
        dmy = sb("dmy", [128, 2], F32)

        with ExitStack() as es2:
            def sb2(name, shape, dt):
                return es2.enter_context(nc.sbuf_tensor(name, list(shape), dt))
            stg = [[sb2(f"stg{a}{i}", [128, 11, 512], F32) for i in range(2)] for a in range(2)]
            obuf = [sb2(f"obuf{i}", [128, 8192], BF16) for i in range(2)]

            items = [
                (DMA(vecs[:], vecd[:, :]), [], [("c", "vecs")]),
                (DMA(gtile[:], gfd.partition_broadcast(128)), [], [("c", "gtile")]),
                (DMA(sk[:], skd.partition_broadcast(128)), [], [("c", "sk")]),
                (DMA(scv[:].rearrange("p a b c -> p (a b c)"), scvd[:, :]), [], [("c", "scv")]),
                (DMA(hst[:].rearrange("p a b -> p (a b)"), hstd[:, :]), [], [("c", "hst")]),
                (DMA(stg[0][0][:, 0:2, :].rearrange("p a b -> p (a b)"), wrgd[:, :]), [], [("stg", 0, 0)]),
                (DMA(stg[1][0][:, 0:2, :].rearrange("p a b -> p (a b)"), wigd[:, :]), [], [("stg", 1, 0)]),
            ]
            P.dma_batch('sp', 'MS', items)
            P.op('pool', MSET(ident[:], 0.0), [], [("c", "ident")])
            P.op('pool', lambda h: h.affine_select(out=ident[:], in_=ident[:], compare_op=ALU.not_equal, fill=1.0,
                                                   base=0, pattern=[[-1, 128]], channel_multiplier=1),
                 [("c", "ident")], [("c", "ident")])
            P.op('pool', MSET(ones_bf[:], 1.0), [], [("c", "ones")])
            P.op('pool', MSET(hcar[:], 0.0), [], [("c", "hcar")])
            P.op('pool', MSET(dmy[:], 1.0), [], [("dmy",)])
            P.op('dve', CP(wrg[:].rearrange("p a b -> p (a b)"), stg[0][0][:, 0:2, :].rearrange("p a b -> p (a b)")),
                 [("stg", 0, 0)], [("c", "wrg")])
            P.op('dve', CP(wig[:].rearrange("p a b -> p (a b)"), stg[1][0][:, 0:2, :].rearrange("p a b -> p (a b)")),
                 [("stg", 1, 0)], [("c", "wig")])
            P.op('act', ACT(dv[:, 32:40], vecs[:, VC_LAM:VC_LAM + 8], AF.Exp, scale=-1.0), [("c", "vecs")], [("c", "dvt")])
            P.op('act', ACT(dv[:, 32:40], dv[:, 32:40], AF.Ln, bias=1.0, scale=1.0), [("c", "dvt")], [("c", "dvt")])
            P.op('dve', TS1(dv[:, 0:8], dv[:, 32:40], -8.0, ALU.mult), [("c", "dvt")], [("c", "dv0")])
            P.op('dve', TS1(dv[:, 8:16], dv[:, 32:40], -16.0, ALU.mult), [("c", "dvt")], [("c", "dv1")])
            P.op('dve', TS1(dv[:, 16:32], vecs[:, VC_BRG:VC_BRG + 16], -1.0, ALU.mult), [("c", "vecs")], [("c", "dv2")])
            P.op('act', ACT(sk[:], sk[:], AF.Exp), [("c", "sk")], [("c", "sk")])
            for Pp in range(2):
                for half in range(2):
                    for g in range(4):
                        idx = (2 * Pp + half) * 4 + g
                        rows = slice(half * 64, half * 64 + 64)
                        P.op('act', ACT(sinkt[rows, Pp, g * 128:(g + 1) * 128], ident[rows, :], AF.Identity,
                                        bias=sk[rows, idx:idx + 1], scale=0.0),
                             [("c", "sk"), ("c", "ident")], [("c", "sinkt", Pp, half, g)])
            for j in range(4):
                for dc in range(8):
                    col = VC_CW + j * 8 + dc
                    P.op('dve', TS1(Dg[:, j, dc, :], ident[:], vecs[:, col:col + 1], ALU.mult),
                         [("c", "vecs"), ("c", "ident")], [("c", "Dg", j, dc)])

            castc = [0]
            cast_engs = ['dve', 'act']
            pend = [None]
            unitc = [0]

            def cast(out, in_, reads, writes):
                e = cast_engs[castc[0] % 2]
                castc[0] += 1
                P.op(e, COPYENG(e, out, in_), reads, writes)

            def unit(loads, casts, dst_ap):
                u = unitc[0] % 2
                unitc[0] += 1
                for a, src, nk, ncols in loads:
                    P.dma('sp', f'PL{a}{u}', DMA(stg[a][u][:, 0:nk, 0:ncols], src), [], [("stg", a, u)])
                if pend[0] is not None:
                    pend[0]()
                    pend[0] = None
                ob = obuf[u]
                for outf, a, inf in casts:
                    cast(outf(ob), inf(stg[a][u]), [("stg", a, u)], [("ob", u)])

                def wr(u=u, ob=ob, dst_ap=dst_ap):
                    P.dma('sp', f'PO{u}', DMA(dst_ap, src_for_dst(ob, dst_ap)), [("ob", u)], [])
                pend[0] = wr

            def src_for_dst(ob, dst_ap):
                shp = dst_ap.shape
                if len(shp) == 2:
                    return ob[:, 0:shp[1]]
                return ob[:, 0:shp[1] * shp[2]].rearrange("p (a b) -> p a b", a=shp[1])

            def rows_view(w, r0, nk, c0, ncols):
                return w[r0:r0 + nk * 128, c0:c0 + ncols].rearrange("(k p) c -> p k c", p=128)

            for f in range(2):
                bgu = B_GU1 if f == 0 else B_GU2
                bd = B_D1 if f == 0 else B_D2
                c0 = 0
                while c0 < FF:
                    ncols = min(512, FF - c0)
                    nch = ncols // 128
                    m0 = c0 // 128
                    casts = []
                    for ch in range(nch):
                        for gu in range(2):
                            casts.append((
                                (lambda ob, ch=ch, gu=gu: ob[:, ch * GU + gu * 1024: ch * GU + (gu + 1) * 1024].rearrange("p (k c) -> p k c", k=8)),
                                gu,
                                (lambda st, ch=ch: st[:, 0:8, ch * 128:(ch + 1) * 128])))
                    off = int(seq_offs[bgu + m0])
                    unit([(0, rows_view(w_g[f], 0, 8, c0, ncols), 8, ncols), (1, rows_view(w_u[f], 0, 8, c0, ncols), 8, ncols)],
                         casts, wscr[:, off:off + nch * GU])
                    c0 += ncols
                for half in range(2):
                    for cg in range(2):
                        casts = []
                        for mm in range(4):
                            casts.append((
                                (lambda ob, mm=mm: ob[:, mm * DH:(mm + 1) * DH].rearrange("p (k c) -> p k c", k=11)),
                                0,
                                (lambda st, mm=mm: st[:, 0:11, mm * 128:(mm + 1) * 128])))
                        off0 = int(seq_offs[bd]) + (cg * 4 * 2 + half) * DH
                        dst = wscr[:, off0:off0 + 4 * 2 * DH].rearrange("p (m x) -> p m x", x=2 * DH)[:, :, 0:DH]
                        unit([(0, rows_view(w_d[f], half * 11 * 128, 11, cg * 512, 512), 11, 512)], casts, dst)
            for un in range(11):
                casts = [((lambda ob, s=s: ob[:, s * WI:(s + 1) * WI].rearrange("p (k c) -> p k c", k=8)), 0,
                          (lambda st, s=s: st[:, 0:8, s * 256:(s + 1) * 256])) for s in range(2)]
                off = int(seq_offs[B_WI + 2 * un])
                unit([(0, rows_view(w_in, 0, 8, un * 512, 512), 8, 512)], casts, wscr[:, off:off + 2 * WI])
            for ra in range(2):
                for cg in range(2):
                    casts = [((lambda ob, s=s: ob[:, s * WI:(s + 1) * WI].rearrange("p (k c) -> p k c", k=8)), 0,
                              (lambda st, s=s: st[:, 0:8, s * 256:(s + 1) * 256])) for s in range(2)]
                    off0 = int(seq_offs[B_BR]) + (2 * (2 * cg) + ra) * WI
                    dst = wscr[:, off0:off0 + 2 * 2 * WI].rearrange("p (m x) -> p m x", x=2 * WI)[:, :, 0:WI]
                    unit([(0, rows_view(w_br, ra * 1024, 8, cg * 512, 512), 8, 512)], casts, dst)
            for cg in range(2):
                casts = [((lambda ob, s=s: ob[:, s * WI:(s + 1) * WI].rearrange("p (k c) -> p k c", k=8)), 0,
                          (lambda st, s=s: st[:, 0:8, s * 256:(s + 1) * 256])) for s in range(2)]
                off = int(seq_offs[B_WO + 2 * cg])
                unit([(0, rows_view(w_o, 0, 8, cg * 512, 512), 8, 512)], casts, wscr[:, off:off + 2 * WI])

            if pend[0] is not None:
                pend[0]()
            P.barrier()

        hT = sb("hT", [128, 8, TP], F32)
        xio = [sb(f"xio{i}", [128, D], F32) for i in range(2)]
        xnT = sb("xnT", [128, 8, TP], BF16)
        xin = [sb(f"xin{i}", [128, D], F32) for i in range(4)]
        R1 = sb("R1", [128, 24, TP], BF16)
        XA = sb("XA", [128, 8, TP + 4], BF16)
        xrh = sb("xrh", [128, 8, 4], BF16)
        qT = sb("qT", [128, 8, TP], BF16)
        kT = sb("kT", [128, 2, 128 + TP], BF16)
        vtok = sb("vtok", [128, 5, 256], BF16)
        recT = sb("recT", [128, 8, TP], BF16)
        attT = sb("attT", [128, 8, TP], BF16)
        tmps = [sb(f"tmp{i}", [128, TP], F32) for i in range(NTMP)]
        sqs = [sb(f"sq{i}", [128, TP], BF16) for i in range(3)]
        xcbs = [sb(f"xcb{i}", [128, TP], BF16) for i in range(2)]
        pTs = [sb(f"pT{i}", [128, TP], BF16) for i in range(8)]
        wring = [sb(f"wr{i}", [128, SLOT], BF16) for i in range(NRING)]
        cst = sb("cst", [128, 8, 4, 3], F32)
        hfin = sb("hfin", [128, 8, 4], F32)
        rtok = sb("rtok", [128, 8], F32)
        kst = sb("kst", [128, 256], F32)
        vst = sb("vst", [128, 256], F32)
        cstT = sb("cstT", [128, 128], F32)
        lstT = sb("lstT", [128, 128], F32)
        kcT = sb("kcT", [128, 4, 2, 128], BF16)
        vc = sb("vc", [128, 4, 256], BF16)
        vsb = sb("vsb", [128, 4, 256], BF16)
        PS = [es.enter_context(nc.psum_tensor(f"ps{i}", [128, 512], F32)) for i in range(8)]

        ctr = {'ps': 0, 'tmp': 0, 'sq': 0, 'xio': 0, 'xcb': 0, 'xin': 0}

        ps_reserved = set()

        def psum(reserve=False):
            while True:
                i = ctr['ps'] % 8
                ctr['ps'] += 1
                if i not in ps_reserved:
                    break
            if reserve:
                ps_reserved.add(i)
            return PS[i], ("ps", i)

        def ps_release(key):
            ps_reserved.discard(key[1])

        def tmp():
            i = ctr['tmp'] % NTMP
            ctr['tmp'] += 1
            return tmps[i], ("tmp", i)

        def sqbuf():
            i = ctr['sq'] % 3
            ctr['sq'] += 1
            return sqs[i], ("sq", i)

        def xcbbuf():
            i = ctr['xcb'] % 2
            ctr['xcb'] += 1
            return xcbs[i], ("xcb", i)

        def xioslot():
            i = ctr['xio'] % 2
            ctr['xio'] += 1
            return i

        P.op('pool', MSET(cst[:], 0.0), [], [("cst", dc) for dc in range(8)])
        P.op('pool', MSET(hfin[:], 0.0), [], [("hfin", dc) for dc in range(8)])
        for i in range(8):
            P.op('pool', MSET(pTs[i][:], 0.0), [], [("pT", i, 0), ("pT", i, 1)])

        ntiles_total = NSEQ * ntile_seq + 1
        total_slabs = ntiles_total * NSLAB
        wst = {'issued': 0, 'cur': 0}

        def w_issue_upto(n):
            while wst['issued'] < min(n, total_slabs):
                i = wst['issued']
                s = i % NRING
                li = i % NSLAB
                sz = seq_sizes[li]
                off = int(seq_offs[li])
                P.dma('sp', f'W{s}', DMA(wring[s][:, 0:sz], wscr[:, off:off + sz]), [], [("wslot", s)])
                wst['issued'] += 1

        def w_next():
            i = wst['cur']
            w_issue_upto(i + 1)
            s = i % NRING
            return wring[s], ("wslot", s)

        def w_get(k):
            i = wst['cur'] + k
            w_issue_upto(i + 1)
            sidx = i % NRING
            return wring[sidx], ("wslot", sidx)

        def w_done():
            wst['cur'] += 1
            w_issue_upto(wst['cur'] + NRING)

        w_issue_upto(NRING)

        class Stats:
            pass

        def stats_begin():
            st = Stats()
            st.ps, st.key = None, None
            st.n = 0
            st.pend = None
            return st

        def stats_add(st, T, dc):
            if st.ps is None:
                st.ps, st.key = psum(reserve=True)
            sq, sqk = sqbuf()
            if st.n % 2 == 0:
                P.op('act', ACT(sq[:, :T], hT[:, dc, :T], AF.Square), [("hT", dc)], [sqk])
            else:
                P.op('dve', TT(sq[:, :T], hT[:, dc, :T], hT[:, dc, :T], ALU.mult), [("hT", dc)], [sqk])
            n = st.n
            st.n += 1
            stats_flush(st)
            st.pend = (lambda: P.group('pe', [MM(st.ps[:, :T], ones_bf[:, :], sq[:, :T], n == 0, n == 7)], [sqk], [st.key]))

        def stats_flush(st):
            if getattr(st, 'pend', None) is not None:
                st.pend()
                st.pend = None

        def norm_finish(st, T, gcol):
            stats_flush(st)
            t1, t1k = tmp()
            rstd, rk = tmp()
            P.op('act', ACT(t1[:, :T], st.ps[:, :T], AF.Ln, bias=EPS, scale=1.0 / D), [st.key], [t1k])
            P.op('act', ACT(rstd[:, :T], t1[:, :T], AF.Exp, scale=-0.5), [t1k], [rk])
            ps_release(st.key)
            for dc in range(8):
                P.op('dve', STT(xnT[:, dc, :T], hT[:, dc, :T], vecs[:, gcol + dc:gcol + dc + 1], rstd[:, :T], ALU.mult, ALU.mult),
                     [("hT", dc), rk], [("xn", dc)])

        def preload_ln():
            P.op('act', ACT(dmy[:, 1:2], dmy[:, 0:1], AF.Ln), [("dmy",)], [("dmy", 1)])

        def norm_fm(T, gcol):
            st = stats_begin()
            for dc in range(8):
                stats_add(st, T, dc)
            norm_finish(st, T, gcol)

        def ffn(T, st=None, fin=False):
            xnk = [("xn", k) for k in range(8)]
            sl = [w_get(0), w_get(1)]
            vv = [sl[i][0][:, 0:GU].rearrange("p (g k c) -> p g k c", g=2, k=8) for i in range(2)]
            pre = [[psum(), psum()] for _ in range(2)]
            for kc in range(8):
                for i in range(2):
                    for gu in range(2):
                        bank, bk = pre[i][gu]
                        P.group('pe', [MM(bank[:, :T], vv[i][:, gu, kc, :], xnT[:, kc, :T], kc == 0, kc == 7)], [sl[i][1], ("xn", kc)], [bk])
            for m in range(FC):
                if m < 2:
                    (pg, pgk), (pu, puk) = pre[m]
                else:
                    slot, wk = w_next()
                    v = slot[:, 0:GU].rearrange("p (g k c) -> p g k c", g=2, k=8)
                    pg, pgk = psum()
                    pu, puk = psum()
                    P.group('pe', [MM(pg[:, :T], v[:, 0, kc, :], xnT[:, kc, :T], kc == 0, kc == 7) for kc in range(8)], [wk] + xnk, [pgk])
                    P.group('pe', [MM(pu[:, :T], v[:, 1, kc, :], xnT[:, kc, :T], kc == 0, kc == 7) for kc in range(8)], [wk] + xnk, [puk])
                w_done()
                sg, sgk = tmp()
                P.op('act', ACT(sg[:, :T], pg[:, :T], AF.Silu), [pgk], [sgk])
                P.op('dve', TT(R1[:, m, :T], pu[:, :T], sg[:, :T], ALU.mult), [puk, sgk], [("R1", m)])
            preload_ln()
            for m in range(8):
                pd, pdk = psum()
                for half in range(2):
                    slot, wk = w_next()
                    v = slot[:, 0:DH].rearrange("p (k c) -> p k c", k=11)
                    P.group('pe', [MM(pd[:, :T], v[:, kk, :], R1[:, half * 11 + kk, :T], (half == 0 and kk == 0), (half == 1 and kk == 10))
                                   for kk in range(11)], [wk] + [("R1", half * 11 + kk) for kk in range(11)], [pdk])
                    w_done()
                P.op('dve', STT(hT[:, m, :T], pd[:, :T], 0.5, hT[:, m, :T], ALU.mult, ALU.add), [pdk, ("hT", m)], [("hT", m)])
                if st is not None:
                    stats_add(st, T, m)
                if fin:
                    if m % 2 == 0:
                        P.op('act', ACT(xnT[:, m, :T], hT[:, m, :T], AF.Square), [("hT", m)], [("xn", m)])
                    else:
                        P.op('dve', TT(xnT[:, m, :T], hT[:, m, :T], hT[:, m, :T], ALU.mult), [("hT", m)], [("xn", m)])
                    P.op('act', ACT(hT[:, m, :T], hT[:, m, :T], AF.Identity, scale=vecs[:, VC_GF + m:VC_GF + m + 1]), [("hT", m)], [("hT", m)])

        def issue_x(T, xsrc, row0):
            TB = min(T, 128)
            slots = []
            for tb in range(T // TB):
                i = ctr['xin'] % 4
                ctr['xin'] += 1
                P.dma('pool', f'XN{i}', DMA(xin[i][:TB, :], xsrc[row0 + tb * TB: row0 + (tb + 1) * TB, :]), [], [("xin", i)])
                slots.append(i)
            return slots

        def load_x(T, slots):
            TB = min(T, 128)
            ssp, ssk = psum(reserve=True)
            pend_mm = [None]
            for tb in range(T // TB):
                i = slots[tb]
                for half in range(2):
                    pt, ptk = psum()
                    fns = [TR(pt[:, j * TB:(j + 1) * TB], xin[i][:TB, (half * 4 + j) * 128:(half * 4 + j + 1) * 128], ident[:TB, :TB]) for j in range(4)]
                    P.group('pe', fns, [("xin", i)], [ptk])
                    hk = [("hT", half * 4 + j) for j in range(4)]
                    P.op('act', ACT(hT[:, half * 4:half * 4 + 4, tb * TB:(tb + 1) * TB], pt[:, 0:4 * TB].rearrange("p (j t) -> p j t", t=TB), AF.Copy),
                         [ptk], hk)
                    sq, sqk = sqbuf()
                    hv = hT[:, half * 4:half * 4 + 4, tb * TB:(tb + 1) * TB]
                    P.op('dve', TT(sq[:, 0:4 * TB].rearrange("p (j t) -> p j t", t=TB), hv, hv, ALU.mult), hk, [sqk])
                    if pend_mm[0] is not None:
                        pend_mm[0]()

                    def mm(sq=sq, sqk=sqk, tb=tb, half=half):
                        P.group('pe', [MM(ssp[:, tb * TB:(tb + 1) * TB], ones_bf[:, :], sq[:, j * TB:(j + 1) * TB], (half == 0 and j == 0), (half == 1 and j == 3))
                                       for j in range(4)], [sqk], [ssk])
                    pend_mm[0] = mm
            st = Stats()
            st.ps, st.key, st.n = ssp, ssk, 8
            st.pend = pend_mm[0]
            return st

        def final_out(T, ydst, row0):
            TB = min(T, 128)
            NBK = T // TB
            pss, pssk = psum()
            for tb in range(NBK):
                P.group('pe', [MM(pss[:TB, tb:tb + 1], xnT[:, dc, tb * TB:(tb + 1) * TB], ones_bf[:, 0:1], dc == 0, dc == 7) for dc in range(8)],
                        [("xn", dc) for dc in range(8)], [pssk])
            P.op('act', ACT(rtok[:TB, 4:4 + NBK], pss[:TB, 0:NBK], AF.Ln, bias=EPS, scale=1.0 / D), [pssk], [("rtok", 1)])
            P.op('act', ACT(rtok[:TB, 0:NBK], rtok[:TB, 4:4 + NBK], AF.Exp, scale=-0.5), [("rtok", 1)], [("rtok", 0)])
            for tb in range(NBK):
                s = xioslot()
                for half in range(2):
                    pt, ptk = psum()
                    fns = [TR(pt[:TB, j * 128:(j + 1) * 128], hT[:, half * 4 + j, tb * TB:(tb + 1) * TB], ident[:, :]) for j in range(4)]
                    P.group('pe', fns, [("hT", half * 4 + j) for j in range(4)], [ptk])
                    if half == 0:
                        P.op('dve', TS1(xio[s][:TB, 0:512], pt[:TB, :], rtok[:TB, tb:tb + 1], ALU.mult), [ptk, ("rtok", 0)], [("xio", s)])
                    else:
                        P.op('act', ACT(xio[s][:TB, 512:1024], pt[:TB, :], AF.Identity, scale=rtok[:TB, tb:tb + 1]), [ptk, ("rtok", 0)], [("xio", s)])
                P.dma('pool', f'XO{s}', DMA(ydst[row0 + tb * TB: row0 + (tb + 1) * TB, :], xio[s][:TB, :]), [("xio", s)], [])

        def small_out_T(src_ap, ncol, stage, stk, sem, dsts):
            pt, ptk = psum()
            P.group('pe', [TR(pt[:ncol, 0:128], src_ap, ident[:, :])], [stk + ("src",)], [ptk])
            P.op('act', ACT(stage[:ncol, :], pt[:ncol, 0:128], AF.Copy), [ptk], [stk])
            P.dma_batch('pool', sem, [(DMA(d, stage[r0:r0 + nr, :]), [stk], []) for d, r0, nr in dsts])

        def mix(T, kind, first, last, seqi, st2=None):
            xnk = [("xn", k) for k in range(8)]
            if kind == 'p':
                segs = [(0, T, 0)]
                LSEG = T
            else:
                segs = [(b * 16, 16, b * 20) for b in range(4)]
                LSEG = 16
            nseg = len(segs)
            need_state = last or kind == 's'

            def xa_new(dc):
                if kind == 'p':
                    return XA[:, dc, 4:4 + T]
                return XA[:, dc, 0:80].rearrange("p (s l) -> p s l", l=20)[:, :, 4:20]

            def ps_seg(p):
                if kind == 'p':
                    return p[:, :T]
                return p[:, 0:64].rearrange("p (s l) -> p s l", l=16)

            def lru_a(dc):
                pc, pck = psum()
                for (t0, L, off) in segs:
                    P.group('pe', [MM(pc[:, t0:t0 + L], Dg[:, j, dc, :], XA[:, dc, off + 1 + j:off + 1 + j + L], j == 0, j == 3) for j in range(4)],
                            [("XA", dc)], [pck])
                xc, xck = tmp()
                P.op('act', ACT(xc[:, :T], pc[:, :T], AF.Identity, bias=vecs[:, VC_CB + dc:VC_CB + dc + 1], scale=1.0), [pck], [xck])
                xcb, xcbk = xcbbuf()
                P.op('dve', CP(xcb[:, :T], xc[:, :T]), [xck], [xcbk])
                return (dc, xc, xck, xcb, xcbk)

            def lru_b(st):
                dc, xc, xck, xcb, xcbk = st
                pr, prk = psum()
                pi, pik = psum()
                P.group('pe', [MM(pr[:, :T], wrg[:, dc, :], xcb[:, :T], True, True)], [xcbk], [prk])
                P.group('pe', [MM(pi[:, :T], wig[:, dc, :], xcb[:, :T], True, True)], [xcbk], [pik])
                r, rk = tmp()
                ig, igk = tmp()
                P.op('act', ACT(r[:, :T], pr[:, :T], AF.Exp, bias=dv[:, 16 + dc:17 + dc], scale=-1.0), [prk], [rk])
                P.op('act', ACT(ig[:, :T], pi[:, :T], AF.Exp, bias=dv[:, 24 + dc:25 + dc], scale=-1.0), [pik], [igk])
                P.op('act', ACT(r[:, :T], r[:, :T], AF.Ln, bias=1.0, scale=1.0), [rk], [rk])
                P.op('act', ACT(ig[:, :T], ig[:, :T], AF.Ln, bias=1.0, scale=1.0), [igk], [igk])
                P.op('act', ACT(r[:, :T], r[:, :T], AF.Exp, scale=-1.0), [rk], [rk])
                P.op('act', ACT(ig[:, :T], ig[:, :T], AF.Exp, scale=-1.0), [igk], [igk])
                a, ak = tmp()
                e2, e2k = tmp()
                P.op('act', ACT(a[:, :T], r[:, :T], AF.Exp, scale=dv[:, dc:dc + 1]), [rk], [ak])
                P.op('act', ACT(e2[:, :T], r[:, :T], AF.Exp, scale=dv[:, 8 + dc:9 + dc]), [rk], [e2k])
                P.op('dve', TT(ig[:, :T], ig[:, :T], xc[:, :T], ALU.mult), [igk, xck], [igk])
                P.op('act', ACT(e2[:, :T], e2[:, :T], AF.Ln, bias=1.0, scale=-1.0), [e2k], [e2k])
                P.op('act', ACT(e2[:, :T], e2[:, :T], AF.Exp, scale=0.5), [e2k], [e2k])
                P.op('dve', TT(ig[:, :T], ig[:, :T], e2[:, :T], ALU.mult), [igk, e2k], [igk])
                hl, hlk = tmp()
                for si, (t0, L, off) in enumerate(segs):
                    if kind == 'p':
                        init = 0.0 if first else hcar[:, dc:dc + 1]
                        rd = [ak, igk] + ([] if first else [("hcar", dc)])
                    else:
                        init = hst[:, dc, si:si + 1]
                        rd = [ak, igk]
                    P.op('dve', SCAN(hl[:, t0:t0 + L], a[:, t0:t0 + L], ig[:, t0:t0 + L], init), rd, [hlk])
                if kind == 'p':
                    P.op('act', ACT(hcar[:, dc:dc + 1], hl[:, T - 1:T], AF.Copy), [hlk], [("hcar", dc)])
                    if last:
                        P.op('act', ACT(hfin[:, dc, 0:1], hl[:, T - 1:T], AF.Copy), [hlk], [("hfin", dc)])
                else:
                    P.op('act', ACT(hfin[:, dc, :], hl[:, 0:64].rearrange("p (s l) -> p s l", l=16)[:, :, 15], AF.Copy), [hlk], [("hfin", dc)])
                P.op('dve', CP(recT[:, dc, :T], hl[:, :T]), [hlk], [("rec", dc)])

            def lru_chunk(dc):
                lru_b(lru_a(dc))

            def v3(ap, g=4):
                return ap.rearrange("p (g q) -> p g q", g=g)
            def attn_A(i):
                qb, Pp = i // 2, i % 2
                buf = i % 2
                blks = []
                if not (first and qb == 0):
                    blks.append((0, qb * 128, qb))
                blks.append((1, (qb + 1) * 128, qb + 1))
                qk = [("q", Pp * 4 + g) for g in range(4)]
                for half in range(2):
                    rows = slice(half * 64, half * 64 + 64)
                    for (role, kc0, vb) in blks:
                        ps_, psk = psum()
                        P.group('pe', [MM(ps_[:, :], kT[rows, Pp, kc0:kc0 + 128], qT[rows, Pp * 4:Pp * 4 + 4, qb * 128:(qb + 1) * 128], True, True)],
                                [("kT", Pp)] + qk, [psk])
                        pi_ = buf * 4 + half * 2 + role
                        pt = pTs[pi_]
                        if role == 0:
                            P.op('act', ACT(v3(pt[0:64, :])[:, :, 0:64], v3(ps_[0:64, :])[:, :, 0:64], AF.Exp, scale=0.125), [psk], [("pT", pi_, 0)])
                            P.op('act', ACT(pt[64:128, :], ps_[64:128, :], AF.Exp, scale=0.125), [psk], [("pT", pi_, 1)])
                        else:
                            P.op('act', ACT(pt[0:64, :], ps_[0:64, :], AF.Exp, scale=0.125), [psk], [("pT", pi_, 0)])
                            P.op('act', ACT(v3(pt[64:128, :])[:, :, 64:128], v3(ps_[64:128, :])[:, :, 64:128], AF.Exp, scale=0.125), [psk], [("pT", pi_, 1)])
                return blks

            def attn_B(i, blks):
                qb, Pp = i // 2, i % 2
                buf = i % 2
                pa, pak = psum()
                pb, pbk = psum()
                for half in range(2):
                    rows = slice(half * 64, half * 64 + 64)
                    kvh = 2 * Pp + half
                    for bi, (role, kc0, vb) in enumerate(blks):
                        pi_ = buf * 4 + half * 2 + role
                        pt = pTs[pi_]
                        ptk = [("pT", pi_, 0), ("pT", pi_, 1)]
                        P.group('pe', [MM(pa[rows, :], vtok[:, vb, kvh * 64:(kvh + 1) * 64], pt[:, :], bi == 0, bi == len(blks) - 1)],
                                [("vt", vb)] + ptk, [pak])
                        P.group('pe', [MM(pb[rows, :], ones_bf[:, 0:64], pt[:, :], bi == 0, bi == len(blks) - 1)], ptk, [pbk])
                den, dk = tmp()
                P.op('dve', TT(den[:, :], pb[:, :], sinkt[:, Pp, :], ALU.add), [pbk], [dk])
                P.op('act', ACT(den[:, :], den[:, :], AF.Ln), [dk], [dk])
                P.op('act', ACT(den[:, :], den[:, :], AF.Exp, scale=-1.0), [dk], [dk])
                P.op('dve', TT(attT[:, Pp * 4:Pp * 4 + 4, qb * 128:(qb + 1) * 128], v3(pa[:, :]), v3(den[:, :]), ALU.mult), [pak, dk],
                     [("att", Pp * 4 + g) for g in range(4)])


            attn_state = {'i': 0, 'prev': None}

            def attn_step():
                st = attn_state
                new = None
                if st['i'] < 8:
                    blks = attn_A(st['i'])
                    new = (st['i'], blks)
                    st['i'] += 1
                if st['prev'] is not None:
                    attn_B(*st['prev'])
                st['prev'] = new

            lru_pend = [None]
            lru_next = [0]
            slw = [w_get(0), w_get(1)]
            vw = [slw[i][0][:, 0:WI].rearrange("p (k c) -> p k c", k=8) for i in range(2)]
            prew = {c: psum() for c in range(4)}
            for kc in range(8):
                for c in range(4):
                    bank, bk = prew[c]
                    P.group('pe', [MM(bank[:, :T], vw[c // 2][:, kc, (c % 2) * 128:(c % 2 + 1) * 128], xnT[:, kc, :T], kc == 0, kc == 7)],
                            [slw[c // 2][1], ("xn", kc)], [bk])

            def lru_step():
                st_new = None
                if lru_next[0] < 8:
                    st_new = lru_a(lru_next[0])
                    lru_next[0] += 1
                if lru_pend[0] is not None:
                    lru_b(lru_pend[0])
                lru_pend[0] = st_new

            for s in range(22):
                stage(3.06 + s * 0.001)
                if 2 <= s <= 10:
                    lru_step()
                if kind == 'p' and s >= 10:
                    attn_step()
                slot, wk = w_next()
                v = slot[:, 0:WI].rearrange("p (k c) -> p k c", k=8)
                if s == 9:
                    if kind == 'p':
                        for tb in range(4):
                            pv, pvk = psum()
                            P.group('pe', [MM(pv[:, 0:256], xnT[:, kc, tb * 128:(tb + 1) * 128], v[:, kc, :], kc == 0, kc == 7)
                                           for kc in range(8)], [wk] + xnk, [pvk])
                            P.op('act', ACT(vtok[:, 1 + tb, :], pv[:, 0:256], AF.Copy), [pvk], [("vt", 1 + tb)])
                            if last and tb == 3 and os.environ.get('KV', '1') == '1':
                                pvo, pvok = psum()
                                P.group('pe', [MM(pvo[:, 0:256], xnT[:, kc, T - 128:T], v[:, kc, :], kc == 0, kc == 7) for kc in range(8)], [wk] + xnk, [pvok])
                                P.op('dve', CP(vst[:, :], pvo[:, 0:256]), [pvok], [("vst",)])
                                P.dma('pool', 'OV', DMA(vpo[seqi, :, :], vst[:, :]), [("vst",)], [])
                    else:
                        pv, pvk = psum()
                        for b in range(2):
                            P.group('pe', [MM(pv[0:16, b * 256:(b + 1) * 256], xnT[:, kc, b * 16:(b + 1) * 16], v[:, kc, :], kc == 0, kc == 7)
                                           for kc in range(8)], [wk] + xnk, [pvk])
                        pv2, pv2k = psum()
                        for b in range(2):
                            P.group('pe', [MM(pv2[0:16, b * 256:(b + 1) * 256], xnT[:, kc, (b + 2) * 16:(b + 3) * 16], v[:, kc, :], kc == 0, kc == 7)
                                           for kc in range(8)], [wk] + xnk, [pv2k])
                        P.op('act', ACT(vsb[0:16, 0:2, :], pv[0:16, :].rearrange("p (j c) -> p j c", j=2), AF.Copy), [pvk], [("vsb", 0)])
                        P.op('act', ACT(vsb[0:16, 2:4, :], pv2[0:16, :].rearrange("p (j c) -> p j c", j=2), AF.Copy), [pv2k], [("vsb", 1)])
                        pv3, pv3k = psum()
                        P.group('pe', [MM(pv3[0:64, 0:256], xnT[:, kc, 0:64], v[:, kc, :], kc == 0, kc == 7) for kc in range(8)], [wk] + xnk, [pv3k])
                        P.op('dve', CP(vst[0:64, :], pv3[0:64, 0:256]), [pv3k], [("vst",)])
                        P.dma('pool', 'OV', DMA(vso[:, :], vst[0:64, :]), [("vst",)], [])
                    w_done()
                    continue
                for j in range(2):
                    c = 2 * s + j
                    if c in prew:
                        p, pk = prew[c]
                    else:
                        p, pk = psum()
                        P.group('pe', [MM(p[:, :T], v[:, kc, j * 128:(j + 1) * 128], xnT[:, kc, :T], kc == 0, kc == 7) for kc in range(8)], [wk] + xnk, [pk])
                    if c < 8:
                        dc = c
                        if kind == 'p':
                            if first:
                                P.op('pool', MSET(XA[:, dc, 0:4], 0.0), [], [("XA", dc)])
                            else:
                                P.op('pool', CP(XA[:, dc, 0:4], xrh[:, dc, :]), [("xrh", dc)], [("XA", dc)])
                        else:
                            P.op('pool', CP(XA[:, dc, 0:80].rearrange("p (s l) -> p s l", l=20)[:, :, 0:4], scv[:, dc, :, :]), [], [("XA", dc)])
                        P.op('act', ACT(xa_new(dc), ps_seg(p), AF.Copy), [pk], [("XA", dc)])
                        if kind == 'p' and not last:
                            P.op('pool', CP(xrh[:, dc, :], XA[:, dc, T:T + 4]), [("XA", dc)], [("xrh", dc)])
                        if need_state:
                            if kind == 'p':
                                P.op('dve', CP(cst[:, dc, 0, :], p[:, T - 3:T]), [pk, ("XA", dc)], [("cst", dc)])
                            else:
                                P.op('dve', CP(cst[:, dc, :, :], p[:, 0:64].rearrange("p (s l) -> p s l", l=16)[:, :, 13:16]), [pk, ("XA", dc)], [("cst", dc)])
                    elif c >= 36:
                        dc = c - 36
                        P.op('act', ACT(R1[:, dc, :T], p[:, :T], AF.Gelu_apprx_tanh), [pk], [("R1", dc)])
                        P.op('dve', TT(recT[:, dc, :T], R1[:, dc, :T], recT[:, dc, :T], ALU.mult), [("R1", dc), ("rec", dc)], [("rec", dc)])
                    elif c < 16:
                        jq = c - 8
                        P.op('dve', CP(qT[:, jq, :T], p[:, :T]), [pk], [("q", jq)])
                    elif c < 18:
                        kc_ = c - 16
                        if kind == 'p':
                            P.op('dve', CP(kT[:, kc_, 128:128 + T], p[:, :T]), [pk], [("kT", kc_)])
                        else:
                            P.op('dve', CP(kT[:, kc_, 0:T], p[:, :T]), [pk], [("kT", kc_)])
                    elif c < 28:
                        dc = c - 20
                        P.op('dve', CP(R1[:, 8 + dc, :T], p[:, :T]), [pk], [("R1", 8 + dc)])
                    else:
                        dc = c - 28
                        P.op('dve', CP(R1[:, 16 + dc, :T], p[:, :T]), [pk], [("R1", 16 + dc)])
                if s == 8 and need_state:
                    pkk, pkkk = psum()
                    if kind == 'p':
                        P.group('pe', [MM(pkk[:, 0:256], xnT[:, kc, T - 128:T], v[:, kc, :], kc == 0, kc == 7) for kc in range(8)], [wk] + xnk, [pkkk])
                        P.op('dve', CP(kst[:, :], pkk[:, 0:256]), [pkkk], [("kst",)])
                        P.dma('pool', 'OK', DMA(kpo[seqi, :, :], kst[:, :]), [("kst",)], [])
                    else:
                        P.group('pe', [MM(pkk[0:64, 0:256], xnT[:, kc, 0:64], v[:, kc, :], kc == 0, kc == 7) for kc in range(8)], [wk] + xnk, [pkkk])
                        P.op('dve', CP(kst[0:64, :], pkk[0:64, 0:256]), [pkkk], [("kst",)])
                        P.dma('pool', 'OK', DMA(kso[:, :], kst[0:64, :]), [("kst",)], [])
                w_done()

            while lru_next[0] < 8 or lru_pend[0] is not None:
                lru_step()
            stage(3.3)
            it = 0
            if kind == 'p':
                while attn_state['i'] < 8 or attn_state['prev'] is not None:
                    attn_step()
                if not last:
                    P.op('pool', CP(kT[:, :, 0:128], kT[:, :, T:T + 128]), [("kT", 0), ("kT", 1)], [("kT", 0), ("kT", 1)])
                    P.op('pool', CP(vtok[:, 0, :], vtok[:, 4, :]), [("vt", 4)], [("vt", 0)])
            else:
                s0 = xioslot()
                P.dma('pool', f'XI{s0}', DMA(xio[s0][:, :].rearrange("k (b f) -> k b f", b=4), ckd.rearrange("b k f -> k b f")), [], [("xio", s0)])
                s1 = xioslot()
                P.dma('pool', f'XI{s1}', DMA(xio[s1][:, :].rearrange("k (b f) -> k b f", b=4), cvd.rearrange("b k f -> k b f")), [], [("xio", s1)])
                for b in range(4):
                    pt, ptk = psum()
                    P.group('pe', [TR(pt[:, c * 128:(c + 1) * 128], xio[s0][:, b * 256 + c * 128: b * 256 + (c + 1) * 128], ident[:, :]) for c in range(2)],
                            [("xio", s0)], [ptk])
                    P.op('dve', CP(kcT[:, b, :, :], pt[:, 0:256].rearrange("p (c k) -> p c k", c=2)), [ptk], [("kcT", b)])
                P.op('act', ACT(vc[:, :, :].rearrange("p b f -> p (b f)"), xio[s1][:, :], AF.Copy), [("xio", s1)], [("vc",)])
                for b in range(4):
                    for Pp in range(2):
                        buf = it % 2
                        it += 1
                        pa, pak = psum()
                        pb, pbk = psum()
                        qk = [("q", Pp * 4 + g) for g in range(4)]
                        for half in range(2):
                            rows = slice(half * 64, half * 64 + 64)
                            kvh = 2 * Pp + half
                            rhs = qT[rows, Pp * 4:Pp * 4 + 4, b * 16:(b + 1) * 16]
                            ps1, ps1k = psum()
                            P.group('pe', [MM(ps1[:, 0:64], kcT[rows, b, Pp, :], rhs, True, True)], [("kcT", b)] + qk, [ps1k])
                            ps2, ps2k = psum()
                            P.group('pe', [MM(ps2[0:16, 0:64], kT[rows, Pp, b * 16:(b + 1) * 16], rhs, True, True)], [("kT", Pp)] + qk, [ps2k])
                            i1 = buf * 4 + half * 2
                            i2 = i1 + 1
                            P.op('act', ACT(pTs[i1][:, 0:64], ps1[:, 0:64], AF.Exp, scale=0.125), [ps1k], [("pT", i1, 0), ("pT", i1, 1)])
                            P.op('act', ACT(pTs[i2][0:16, 0:64], ps2[0:16, 0:64], AF.Exp, scale=0.125), [ps2k], [("pT", i2, 0), ("pT", i2, 1)])
                            k1 = [("pT", i1, 0), ("pT", i1, 1)]
                            k2 = [("pT", i2, 0), ("pT", i2, 1)]
                            P.group('pe', [MM(pa[rows, 0:64], vc[:, b, kvh * 64:(kvh + 1) * 64], pTs[i1][:, 0:64], True, False)], [("vc",)] + k1, [pak])
                            P.group('pe', [MM(pa[rows, 0:64], vsb[0:16, b, kvh * 64:(kvh + 1) * 64], pTs[i2][0:16, 0:64], False, True)],
                                    [("vsb", b // 2)] + k2, [pak])
                            P.group('pe', [MM(pb[rows, 0:64], ones_bf[:, 0:64], pTs[i1][:, 0:64], True, False)], k1, [pbk])
                            P.group('pe', [MM(pb[rows, 0:64], ones_bf[0:16, 0:64], pTs[i2][0:16, 0:64], False, True)], k2, [pbk])
                        den, dk = tmp()
                        P.op('dve', TT(v3(den[:, 0:64]), v3(pb[:, 0:64]), v3(sinkt[:, Pp, :])[:, :, 0:16], ALU.add), [pbk], [dk])
                        P.op('dve', RECIP(den[:, 0:64], den[:, 0:64]), [dk], [dk])
                        P.op('dve', TT(attT[:, Pp * 4:Pp * 4 + 4, b * 16:(b + 1) * 16], v3(pa[:, 0:64]), v3(den[:, 0:64]), ALU.mult), [pak, dk],
                             [("att", Pp * 4 + g) for g in range(4)])

            stage(3.2)
            if need_state:
                ncs = 12
                nsq = 4
                cdst = cpo if kind == 'p' else cso
                ldst = lpo if kind == 'p' else lso
                pt, ptk = psum()
                P.group('pe', [TR(pt[:96, 0:128], cst[:, :, :, :].rearrange("p a b c -> p (a b c)"), ident[:, :])],
                        [("cst", dc) for dc in range(8)], [ptk])
                P.op('act', ACT(cstT[:8 * ncs, :], pt[:8 * ncs, 0:128], AF.Copy), [ptk], [("cstT",)])
                items = []
                for dc in range(8):
                    if kind == 'p':
                        d = cdst[seqi, :, dc * 128:(dc + 1) * 128]
                    else:
                        d = cdst[:, :, dc * 128:(dc + 1) * 128].rearrange("s r p -> (s r) p")
                    items.append((DMA(d, cstT[dc * 12:dc * 12 + 3 * nseg, :]), [("cstT",)], []))
                P.dma_batch('pool', 'OC', items)
                pt2, pt2k = psum()
                P.group('pe', [TR(pt2[:32, 0:128], hfin[:, :, :].rearrange("p a b -> p (a b)"), ident[:, :])],
                        [("hfin", dc) for dc in range(8)], [pt2k])
                P.op('act', ACT(lstT[:32, :], pt2[:32, 0:128], AF.Copy), [pt2k], [("lstT",)])
                items = []
                for dc in range(8):
                    if kind == 'p':
                        d = ldst[seqi:seqi + 1, dc * 128:(dc + 1) * 128]
                    else:
                        d = ldst[:, dc * 128:(dc + 1) * 128]
                    items.append((DMA(d, lstT[dc * 4:dc * 4 + nseg, :]), [("lstT",)], []))
                P.dma_batch('pool', 'OL', items)

            stage(3.4)
            for m in range(8):
                P.op('act', ACT(R1[:, 8 + m, :T], R1[:, 8 + m, :T], AF.Tanh, scale=0.5), [("R1", 8 + m)], [("R1", 8 + m)])
                P.op('act', ACT(R1[:, 16 + m, :T], R1[:, 16 + m, :T], AF.Tanh, scale=0.5), [("R1", 16 + m)], [("R1", 16 + m)])
            reck = [("rec", k) for k in range(8)]
            attk = [("att", k) for k in range(8)]
            for sidx in range(4):
                pbr = []
                slot, wk = w_next()
                v = slot[:, 0:WI].rearrange("p (k c) -> p k c", k=8)
                for j in range(2):
                    p, pk = psum()
                    P.group('pe', [MM(p[:, :T], v[:, kc, j * 128:(j + 1) * 128], recT[:, kc, :T], kc == 0, kc == 7) for kc in range(8)], [wk] + reck, [pk])
                    pbr.append((p, pk))
                w_done()
                slot, wk = w_next()
                v = slot[:, 0:WI].rearrange("p (k c) -> p k c", k=8)
                pba = []
                for j in range(2):
                    p, pk = psum()
                    P.group('pe', [MM(p[:, :T], v[:, kc, j * 128:(j + 1) * 128], attT[:, kc, :T], kc == 0, kc == 7) for kc in range(8)], [wk] + attk, [pk])
                    pba.append((p, pk))
                w_done()
                for j in range(2):
                    m = sidx * 2 + j
                    m1, m1k = tmp()
                    m2, m2k = tmp()
                    P.op('dve', STT(m1[:, :T], R1[:, 8 + m, :T], 1.0, pbr[j][0][:, :T], ALU.add, ALU.mult), [("R1", 8 + m), pbr[j][1]], [m1k])
                    P.op('dve', STT(m2[:, :T], R1[:, 16 + m, :T], 1.0, pba[j][0][:, :T], ALU.add, ALU.mult), [("R1", 16 + m), pba[j][1]], [m2k])
                    P.op('dve', TT(qT[:, m, :T], m1[:, :T], m2[:, :T], ALU.add), [m1k, m2k], [("q", m)])
            preload_ln()
            mk = [("q", k) for k in range(8)]
            for sidx in range(4):
                slot, wk = w_next()
                v = slot[:, 0:WI].rearrange("p (k c) -> p k c", k=8)
                for j in range(2):
                    m = sidx * 2 + j
                    po, pok = psum()
                    P.group('pe', [MM(po[:, :T], v[:, kc, j * 128:(j + 1) * 128], qT[:, kc, :T], kc == 0, kc == 7) for kc in range(8)], [wk] + mk, [pok])
                    P.op('dve', STT(hT[:, m, :T], po[:, :T], 0.5, hT[:, m, :T], ALU.mult, ALU.add), [pok, ("hT", m)], [("hT", m)])
                    if st2 is not None:
                        stats_add(st2, T, m)
                w_done()

        import os
        KSTOP = float(os.environ.get("KSTOP", "99"))

        class _Stop(Exception):
            pass

        def stage(n):
            if KSTOP <= n:
                raise _Stop()

        tiles = []
        for seqi in range(NSEQ):
            for t in range(ntile_seq):
                tiles.append(('p', TP, xp, yp, seqi * LP + t * TP, t == 0, t == ntile_seq - 1, seqi))
        tiles.append(('s', TS, xs, ys, 0, True, True, 0))

        def tile(idx, slots):
            kind, T, xsrc, ydst, row0, first, last, seqi = tiles[idx]
            stage(0)
            st1 = load_x(T, slots)
            stage(1)
            norm_finish(st1, T, VC_G1)
            stage(2)
            stm = stats_begin()
            ffn(T, stm)
            stage(3)
            norm_finish(stm, T, VC_GM)
            stage(3.05)
            st2 = stats_begin()
            mix(T, kind, first, last, seqi, st2)
            stage(4)
            norm_finish(st2, T, VC_G2)
            nxt = None
            if idx + 1 < len(tiles):
                nk, nT, nx, ny, nr0 = tiles[idx + 1][:5]
                nxt = issue_x(nT, nx, nr0)
            ffn(T, None, True)
            final_out(T, ydst, row0)
            stage(5)
            return nxt

        try:
            slots = issue_x(tiles[0][1], tiles[0][2], tiles[0][4])
            for idx in range(len(tiles)):
                slots = tile(idx, slots)
        except _Stop:
            pass

        P.final_wait('sp')
        P.final_wait('pool')

        for name in sorted(P.semnames):
            P.sems[name] = es.enter_context(nc.semaphore(name))
        block = es.enter_context(nc.Block())

        @block.tensor
        def _(h):
            for f in P.prog['pe']:
                f(h)

        @block.scalar
        def _(h):
            for f in P.prog['act']:
                f(h)

        @block.vector
        def _(h):
            for f in P.prog['dve']:
                f(h)

        @block.gpsimd
        def _(h):
            for f in P.prog['pool']:
                f(h)

        @block.sync
        def _(h):
            for f in P.prog['sp']:
                f(h)
    return nc


def _qperm():
    idx = []
    for Pp in range(2):
        for g in range(4):
            for half in range(2):
                head = (2 * Pp + half) * 4 + g
                idx.extend(range(head * 64, head * 64 + 64))
    return np.array(idx, dtype=np.int64)


def _prep_shared(inp):
    f = lambda a: np.ascontiguousarray(np.asarray(a, dtype=np.float32))
    qp = _qperm()
    w_in = np.asarray(inp['w_in'][0], dtype=np.float32)
    cols = np.concatenate([np.arange(0, 1024), 2048 + qp, np.arange(3072, 3328), np.arange(3328, 3584),
                           np.arange(3584, 4608), np.arange(4608, 5632), np.arange(1024, 2048)])
    w_in_p = f(w_in[:, cols])
    wb = np.asarray(inp['w_branch'][0], dtype=np.float32)
    rows = np.arange(2048)
    rows[1024:2048] = 1024 + qp
    wb_p = f(wb[rows, :])

    def fm(vec):
        return np.asarray(vec, dtype=np.float32).reshape(8, 128).T
    vecs = np.zeros((128, NVEC), np.float32)
    vecs[:, VC_G1:VC_G1 + 8] = fm(inp['norm_ff1'][0])
    vecs[:, VC_GM:VC_GM + 8] = fm(inp['norm_mix'][0])
    vecs[:, VC_G2:VC_G2 + 8] = fm(inp['norm_ff2'][0])
    vecs[:, VC_CB:VC_CB + 8] = fm(inp['conv_b'][0])
    vecs[:, VC_BRG:VC_BRG + 8] = np.asarray(inp['b_rg'][0], np.float32).T
    vecs[:, VC_BIG:VC_BIG + 8] = np.asarray(inp['b_ig'][0], np.float32).T
    vecs[:, VC_LAM:VC_LAM + 8] = fm(inp['lru_lambda'][0])
    vecs[:, VC_GF:VC_GF + 8] = fm(inp['norm_final'])
    cw = np.asarray(inp['conv_w'][0], np.float32)
    for j in range(4):
        vecs[:, VC_CW + j * 8:VC_CW + (j + 1) * 8] = fm(cw[j])
    wrg = f(np.asarray(inp['w_rg'][0], np.float32).transpose(1, 0, 2).reshape(128, 1024))
    wig = f(np.asarray(inp['w_ig'][0], np.float32).transpose(1, 0, 2).reshape(128, 1024))
    return {
        'vecs': vecs, 'gf': f(inp['norm_final']), 'sinks': f(inp['attn_sinks'][0]),
        'wrg': wrg, 'wig': wig,
        'ff1_gate': f(inp['ff1_gate'][0]), 'ff1_up': f(inp['ff1_up'][0]), 'ff1_down': f(inp['ff1_down'][0]),
        'ff2_gate': f(inp['ff2_gate'][0]), 'ff2_up': f(inp['ff2_up'][0]), 'ff2_down': f(inp['ff2_down'][0]),
        'w_in': w_in_p, 'w_branch': wb_p, 'w_out': f(inp['w_out'][0]),
    }


def run_step(inp, seq_len):
    ntile_seq = seq_len // TP
    xpf = np.asarray(inp['x_prompt'], np.float32)
    xsf = np.asarray(inp['x_sample'], np.float32)
    B = xpf.shape[0]
    assert B == 2 * NCORES and xsf.shape[0] == 4 * NCORES
    shared = _prep_shared(inp)
    ck = np.asarray(inp['cache_k'][0], np.float32).reshape(32, 128, 256)
    cv = np.asarray(inp['cache_v'][0], np.float32).reshape(32, 128, 256)
    sconv = np.asarray(inp['state_conv'][0], np.float32)
    slru = np.asarray(inp['state_lru'][0], np.float32)
    in_maps = []
    for c in range(NCORES):
        m = dict(shared)
        m['xp'] = np.ascontiguousarray(xpf[2 * c:2 * c + 2, :seq_len].reshape(2 * seq_len, D))
        m['xs'] = np.ascontiguousarray(xsf[4 * c:4 * c + 4].reshape(TS, D))
        m['ck'] = np.ascontiguousarray(ck[4 * c:4 * c + 4])
        m['cv'] = np.ascontiguousarray(cv[4 * c:4 * c + 4])
        sc = sconv[4 * c:4 * c + 4]
        scp = np.zeros((128, 8, 4, 4), np.float32)
        scp[:, :, :, 1:4] = sc.reshape(4, 3, 8, 128).transpose(3, 2, 0, 1)
        m['scv'] = np.ascontiguousarray(scp.reshape(128, 128))
        sl = slru[4 * c:4 * c + 4]
        m['hst'] = np.ascontiguousarray(sl.reshape(4, 8, 128).transpose(2, 1, 0).reshape(128, 32))
        in_maps.append(m)
    nc = build_program(ntile_seq)
    res = run_bass_kernel_spmd(nc, in_maps, core_ids=list(range(NCORES)))
    R = res.results
    y_prompt = np.stack([R[c]['yp'].reshape(2, seq_len, D) for c in range(NCORES)]).reshape(B, seq_len, D)
    y_sample = np.stack([R[c]['ys'].reshape(4, 16, D) for c in range(NCORES)]).reshape(32, 16, D)
    kp = np.concatenate([R[c]['kp'] for c in range(NCORES)]).reshape(1, B, 128, 4, 64)
    vp = np.concatenate([R[c]['vp'] for c in range(NCORES)]).reshape(1, B, 128, 4, 64)
    cp = np.concatenate([R[c]['cp'] for c in range(NCORES)]).reshape(1, B, 3, D)
    lp = np.concatenate([R[c]['lp'] for c in range(NCORES)]).reshape(1, B, D)
    ks = np.concatenate([R[c]['ks'].reshape(4, 16, 256) for c in range(NCORES)]).reshape(1, 32, 16, 4, 64)
    vs = np.concatenate([R[c]['vs'].reshape(4, 16, 256) for c in range(NCORES)]).reshape(1, 32, 16, 4, 64)
    cs = np.concatenate([R[c]['cs'] for c in range(NCORES)]).reshape(1, 32, 3, D)
    ls = np.concatenate([R[c]['ls'] for c in range(NCORES)]).reshape(1, 32, D)
    outs = (y_prompt, y_sample, kp, vp, cp, lp, ks, vs, cs, ls)
    return tuple(np.ascontiguousarray(o.astype(np.float32)) for o in outs)


def kernel(**inputs):
    return run_step(inputs, 4096)
```

```python
import os
import numpy as np
from contextlib import ExitStack
import concourse.bass as bass
import concourse.mybir as mybir
from concourse.bass_utils import run_bass_kernel_spmd

F32 = mybir.dt.float32
BF16 = mybir.dt.bfloat16
AF = mybir.ActivationFunctionType
ALU = mybir.AluOpType

NCORES = 8
D = 1024
FF = 2816
FC = 22
TP = 512
TS = 64
EPS = 1e-6
NRING = 7
SLOT = 2048
NTMP = 12
GU, DH, WI = 2048, 1408, 2048

ENG_SEM = {'pe': 'S_pe', 'act': 'S_act', 'dve': 'S_dve', 'pool': 'S_pool'}


class Planner:
    def __init__(self, same_sync=True):
        self.prog = {e: [] for e in ('pe', 'act', 'dve', 'pool', 'sp')}
        self.cnt = {}
        self.lastw = {}
        self.readers = {}
        self.waited = {e: {} for e in self.prog}
        self.same_sync = same_sync
        self.sems = {}
        self.semnames = set(ENG_SEM.values())

    def _collect(self, reads, writes):
        deps = {}

        def add(d):
            if d is None:
                return
            s, v = d
            if deps.get(s, 0) < v:
                deps[s] = v
        for k in reads:
            add(self.lastw.get(k))
        for k in writes:
            add(self.lastw.get(k))
            for s, v in self.readers.get(k, {}).items():
                add((s, v))
        return deps

    def _waits(self, eng, deps, force_own=False):
        own = ENG_SEM.get(eng)
        out = []
        for s, v in deps.items():
            if s == own and not force_own and (eng == 'pe' or not self.same_sync):
                continue
            if self.waited[eng].get(s, 0) >= v:
                continue
            self.waited[eng][s] = v
            out.append((s, v))
        return out

    def _record(self, reads, writes, sem, val):
        for k in reads:
            r = self.readers.setdefault(k, {})
            if r.get(sem, 0) < val:
                r[sem] = val
        for k in writes:
            self.lastw[k] = (sem, val)
            self.readers[k] = {}

    def op(self, eng, fn, reads=(), writes=()):
        self.group(eng, [fn], reads, writes)

    def group(self, eng, fns, reads=(), writes=()):
        sem = ENG_SEM[eng]
        waits = self._waits(eng, self._collect(reads, writes))
        val = self.cnt.get(sem, 0) + 1
        self.cnt[sem] = val
        sems = self.sems

        def emit(h, waits=waits, fns=fns, sem=sem):
            for s, v in waits:
                h.wait_ge(sems[s], v)
            for f in fns[:-1]:
                f(h)
            fns[-1](h).then_inc(sems[sem], 1)
        self.prog[eng].append(emit)
        self._record(reads, writes, sem, val)

    def dma(self, q, sem, fn, reads=(), writes=()):
        self.dma_batch(q, sem, [(fn, reads, writes)])

    def dma_batch(self, q, sem, items):
        self.semnames.add(sem)
        base = self.cnt.get(sem, 0)
        total = base + 16 * len(items)
        self.cnt[sem] = total
        sems = self.sems
        for fn, reads, writes in items:
            waits = self._waits(q, self._collect(reads, writes))

            def emit(h, waits=waits, fn=fn, sem=sem):
                for s, v in waits:
                    h.wait_ge(sems[s], v)
                fn(h).then_inc(sems[sem], 16)
            self.prog[q].append(emit)
        for fn, reads, writes in items:
            self._record(reads, writes, sem, total)

    def barrier(self):
        snap = dict(self.cnt)
        for e in self.prog:
            waits = self._waits(e, snap, force_own=(e != 'pe'))
            sems = self.sems

            def emit(h, waits=waits):
                for s, v in waits:
                    h.wait_ge(sems[s], v)
            self.prog[e].append(emit)
        self.lastw.clear()
        self.readers.clear()

    def final_wait(self, eng):
        snap = dict(self.cnt)
        waits = self._waits(eng, snap, force_own=False)
        sems = self.sems

        def emit(h, waits=waits):
            for s, v in waits:
                h.wait_ge(sems[s], v)
        self.prog[eng].append(emit)


def MM(out, lhsT, rhs, start, stop):
    return lambda h: h.matmul(out, lhsT=lhsT, rhs=rhs, start=start, stop=stop)


def TR(out, in_, ident):
    return lambda h: h.transpose(out, in_, ident)


def ACT(out, in_, func, bias=None, scale=None):
    kw = {}
    if bias is not None:
        kw['bias'] = bias
    if scale is not None:
        kw['scale'] = scale
    return lambda h: h.activation(out=out, in_=in_, func=func, **kw)


def TT(out, a, b, op):
    return lambda h: h.tensor_tensor(out=out, in0=a, in1=b, op=op)


def TS1(out, a, s1, op0):
    return lambda h: h.tensor_scalar(out=out, in0=a, scalar1=s1, scalar2=None, op0=op0)


def STT(out, in0, scalar, in1, op0, op1):
    return lambda h: h.scalar_tensor_tensor(out=out, in0=in0, scalar=scalar, in1=in1, op0=op0, op1=op1)


def CP(out, in_):
    return lambda h: h.tensor_copy(out=out, in_=in_)


def RECIP(out, in_):
    return lambda h: h.reciprocal(out=out, in_=in_)


def SCAN(out, d0, d1, init):
    return lambda h: h.tensor_tensor_scan(out=out, data0=d0, data1=d1, initial=init, op0=ALU.mult, op1=ALU.add)


def MSET(out, v):
    return lambda h: h.memset(out, v)


def DMA(out, in_):
    return lambda h: h.dma_start(out=out, in_=in_)


def COPYENG(eng, out, in_):
    if eng == 'act':
        return ACT(out, in_, AF.Copy)
    return CP(out, in_)


def slab_sizes():
    seq = [GU] * FC + [DH] * 16 + [WI] * 22 + [WI] * 8 + [WI] * 4 + [GU] * FC + [DH] * 16
    offs = np.concatenate([[0], np.cumsum(seq)]).astype(np.int64)
    return seq, offs


VC_G1, VC_GM, VC_G2, VC_CB, VC_BRG, VC_BIG, VC_LAM, VC_CW = 0, 8, 16, 24, 32, 40, 48, 56
NVEC = 88


def build_program(ntile_seq):
    nc = bass.Bass("TRN2", target_bir_lowering=False)
    P = Planner(same_sync=True)
    NSEQ = 2
    LP = ntile_seq * TP

    def din(name, shape, dt=F32):
        return nc.dram_tensor(name, list(shape), dt, kind="ExternalInput").ap()

    def dout(name, shape, dt=F32):
        return nc.dram_tensor(name, list(shape), dt, kind="ExternalOutput").ap()

    xp = din("xp", [NSEQ * LP, D])
    xs = din("xs", [TS, D])
    ckd = din("ck", [4, 128, 256])
    cvd = din("cv", [4, 128, 256])
    scvd = din("scv", [128, 8 * 4 * 4])
    hstd = din("hst", [128, 32])
    vecd = din("vecs", [128, NVEC])
    gfd = din("gf", [D])
    skd = din("sinks", [16])
    wrgd = din("wrg", [128, 1024])
    wigd = din("wig", [128, 1024])
    w_g = [din("ff1_gate", [D, FF]), din("ff2_gate", [D, FF])]
    w_u = [din("ff1_up", [D, FF]), din("ff2_up", [D, FF])]
    w_d = [din("ff1_down", [FF, D]), din("ff2_down", [FF, D])]
    w_in = din("w_in", [D, 5632])
    w_br = din("w_branch", [2048, D])
    w_o = din("w_out", [D, D])

    yp = dout("yp", [NSEQ * LP, D])
    ys = dout("ys", [TS, D])
    kpo = dout("kp", [NSEQ, 128, 256])
    vpo = dout("vp", [NSEQ, 128, 256])
    cpo = dout("cp", [NSEQ, 3, D])
    lpo = dout("lp", [NSEQ, D])
    kso = dout("ks", [TS, 256])
    vso = dout("vs", [TS, 256])
    cso = dout("cs", [4, 3, D])
    lso = dout("ls", [4, D])

    seq_sizes, seq_offs = slab_sizes()
    NSLAB = len(seq_sizes)
    TOT = int(seq_offs[-1]) + 8192
    wscr = nc.dram_tensor("wscr", [128, TOT], BF16, kind="Internal").ap()
    B_GU1, B_D1, B_WI, B_BR, B_WO, B_GU2, B_D2 = 0, 22, 38, 60, 68, 72, 94

    with ExitStack() as es:
        def sb(name, shape, dt):
            return es.enter_context(nc.sbuf_tensor(name, list(shape), dt))

        ident = sb("ident", [128, 128], F32)
        ones_bf = sb("ones_bf", [128, 128], BF16)
        vecs = sb("vecs_sb", [128, NVEC], F32)
        dv = sb("dv", [128, 40], F32)
        gtile = sb("gtile", [128, D], F32)
        sk = sb("sk", [128, 16], F32)
        sinkt = sb("sinkt", [128, 2, 512], F32)
        Dg = sb("Dg", [128, 4, 8, 128], BF16)
        wrg = sb("wrg_sb", [128, 8, 128], BF16)
        wig = sb("wig_sb", [128, 8, 128], BF16)
        hcar = sb("hcar", [128, 8], F32)
        scv = sb("scv_sb", [128, 8, 4, 4], F32)
        hst = sb("hst_sb", [128, 8, 4], F32)
        dmy = sb("dmy", [128, 2], F32)

        with ExitStack() as es2:
            def sb2(name, shape, dt):
                return es2.enter_context(nc.sbuf_tensor(name, list(shape), dt))
            stg = [[sb2(f"stg{a}{i}", [128, 11, 512], F32) for i in range(2)] for a in range(2)]
            obuf = [sb2(f"obuf{i}", [128, 8192], BF16) for i in range(2)]

            items = [
                (DMA(vecs[:], vecd[:, :]), [], [("c", "vecs")]),
                (DMA(gtile[:], gfd.partition_broadcast(128)), [], [("c", "gtile")]),
                (DMA(sk[:], skd.partition_broadcast(128)), [], [("c", "sk")]),
                (DMA(scv[:].rearrange("p a b c -> p (a b c)"), scvd[:, :]), [], [("c", "scv")]),
                (DMA(hst[:].rearrange("p a b -> p (a b)"), hstd[:, :]), [], [("c", "hst")]),
                (DMA(stg[0][0][:, 0:2, :].rearrange("p a b -> p (a b)"), wrgd[:, :]), [], [("stg", 0, 0)]),
                (DMA(stg[1][0][:, 0:2, :].rearrange("p a b -> p (a b)"), wigd[:, :]), [], [("stg", 1, 0)]),
            ]
            P.dma_batch('sp', 'MS', items)
            P.op('pool', MSET(ident[:], 0.0), [], [("c", "ident")])
            P.op('pool', lambda h: h.affine_select(out=ident[:], in_=ident[:], compare_op=ALU.not_equal, fill=1.0,
                                                   base=0, pattern=[[-1, 128]], channel_multiplier=1),
                 [("c", "ident")], [("c", "ident")])
            P.op('pool', MSET(ones_bf[:], 1.0), [], [("c", "ones")])
            P.op('pool', MSET(hcar[:], 0.0), [], [("c", "hcar")])
            P.op('pool', MSET(dmy[:], 1.0), [], [("dmy",)])
            P.op('dve', CP(wrg[:].rearrange("p a b -> p (a b)"), stg[0][0][:, 0:2, :].rearrange("p a b -> p (a b)")),
                 [("stg", 0, 0)], [("c", "wrg")])
            P.op('dve', CP(wig[:].rearrange("p a b -> p (a b)"), stg[1][0][:, 0:2, :].rearrange("p a b -> p (a b)")),
                 [("stg", 1, 0)], [("c", "wig")])
            P.op('act', ACT(dv[:, 32:40], vecs[:, VC_LAM:VC_LAM + 8], AF.Exp, scale=-1.0), [("c", "vecs")], [("c", "dvt")])
            P.op('act', ACT(dv[:, 32:40], dv[:, 32:40], AF.Ln, bias=1.0, scale=1.0), [("c", "dvt")], [("c", "dvt")])
            P.op('dve', TS1(dv[:, 0:8], dv[:, 32:40], -8.0, ALU.mult), [("c", "dvt")], [("c", "dv0")])
            P.op('dve', TS1(dv[:, 8:16], dv[:, 32:40], -16.0, ALU.mult), [("c", "dvt")], [("c", "dv1")])
            P.op('dve', TS1(dv[:, 16:32], vecs[:, VC_BRG:VC_BRG + 16], -1.0, ALU.mult), [("c", "vecs")], [("c", "dv2")])
            P.op('act', ACT(sk[:], sk[:], AF.Exp), [("c", "sk")], [("c", "sk")])
            for Pp in range(2):
                for half in range(2):
                    for g in range(4):
                        idx = (2 * Pp + half) * 4 + g
                        rows = slice(half * 64, half * 64 + 64)
                        P.op('act', ACT(sinkt[rows, Pp, g * 128:(g + 1) * 128], ident[rows, :], AF.Identity,
                                        bias=sk[rows, idx:idx + 1], scale=0.0),
                             [("c", "sk"), ("c", "ident")], [("c", "sinkt", Pp, half, g)])
            for j in range(4):
                for dc in range(8):
                    col = VC_CW + j * 8 + dc
                    P.op('dve', TS1(Dg[:, j, dc, :], ident[:], vecs[:, col:col + 1], ALU.mult),
                         [("c", "vecs"), ("c", "ident")], [("c", "Dg", j, dc)])

            castc = [0]
            cast_engs = ['dve', 'act']
            pend = [None]
            unitc = [0]

            def cast(out, in_, reads, writes):
                e = cast_engs[castc[0] % 2]
                castc[0] += 1
                P.op(e, COPYENG(e, out, in_), reads, writes)

            def unit(loads, casts, dst_ap):
                u = unitc[0] % 2
                unitc[0] += 1
                for a, src, nk, ncols in loads:
                    P.dma('sp', f'PL{a}{u}', DMA(stg[a][u][:, 0:nk, 0:ncols], src), [], [("stg", a, u)])
                if pend[0] is not None:
                    pend[0]()
                    pend[0] = None
                ob = obuf[u]
                for outf, a, inf in casts:
                    cast(outf(ob), inf(stg[a][u]), [("stg", a, u)], [("ob", u)])

                def wr(u=u, ob=ob, dst_ap=dst_ap):
                    P.dma('sp', f'PO{u}', DMA(dst_ap, src_for_dst(ob, dst_ap)), [("ob", u)], [])
                pend[0] = wr

            def src_for_dst(ob, dst_ap):
                shp = dst_ap.shape
                if len(shp) == 2:
                    return ob[:, 0:shp[1]]
                return ob[:, 0:shp[1] * shp[2]].rearrange("p (a b) -> p a b", a=shp[1])

            def rows_view(w, r0, nk, c0, ncols):
                return w[r0:r0 + nk * 128, c0:c0 + ncols].rearrange("(k p) c -> p k c", p=128)

            for f in range(2):
                bgu = B_GU1 if f == 0 else B_GU2
                bd = B_D1 if f == 0 else B_D2
                c0 = 0
                while c0 < FF:
                    ncols = min(512, FF - c0)
                    nch = ncols // 128
                    m0 = c0 // 128
                    casts = []
                    for ch in range(nch):
                        for gu in range(2):
                            casts.append((
                                (lambda ob, ch=ch, gu=gu: ob[:, ch * GU + gu * 1024: ch * GU + (gu + 1) * 1024].rearrange("p (k c) -> p k c", k=8)),
                                gu,
                                (lambda st, ch=ch: st[:, 0:8, ch * 128:(ch + 1) * 128])))
                    off = int(seq_offs[bgu + m0])
                    unit([(0, rows_view(w_g[f], 0, 8, c0, ncols), 8, ncols), (1, rows_view(w_u[f], 0, 8, c0, ncols), 8, ncols)],
                         casts, wscr[:, off:off + nch * GU])
                    c0 += ncols
                for half in range(2):
                    for cg in range(2):
                        casts = []
                        for mm in range(4):
                            casts.append((
                                (lambda ob, mm=mm: ob[:, mm * DH:(mm + 1) * DH].rearrange("p (k c) -> p k c", k=11)),
                                0,
                                (lambda st, mm=mm: st[:, 0:11, mm * 128:(mm + 1) * 128])))
                        off0 = int(seq_offs[bd]) + (cg * 4 * 2 + half) * DH
                        dst = wscr[:, off0:off0 + 4 * 2 * DH].rearrange("p (m x) -> p m x", x=2 * DH)[:, :, 0:DH]
                        unit([(0, rows_view(w_d[f], half * 11 * 128, 11, cg * 512, 512), 11, 512)], casts, dst)
            for un in range(11):
                casts = [((lambda ob, s=s: ob[:, s * WI:(s + 1) * WI].rearrange("p (k c) -> p k c", k=8)), 0,
                          (lambda st, s=s: st[:, 0:8, s * 256:(s + 1) * 256])) for s in range(2)]
                off = int(seq_offs[B_WI + 2 * un])
                unit([(0, rows_view(w_in, 0, 8, un * 512, 512), 8, 512)], casts, wscr[:, off:off + 2 * WI])
            for ra in range(2):
                for cg in range(2):
                    casts = [((lambda ob, s=s: ob[:, s * WI:(s + 1) * WI].rearrange("p (k c) -> p k c", k=8)), 0,
                              (lambda st, s=s: st[:, 0:8, s * 256:(s + 1) * 256])) for s in range(2)]
                    off0 = int(seq_offs[B_BR]) + (2 * (2 * cg) + ra) * WI
                    dst = wscr[:, off0:off0 + 2 * 2 * WI].rearrange("p (m x) -> p m x", x=2 * WI)[:, :, 0:WI]
                    unit([(0, rows_view(w_br, ra * 1024, 8, cg * 512, 512), 8, 512)], casts, dst)
            for cg in range(2):
                casts = [((lambda ob, s=s: ob[:, s * WI:(s + 1) * WI].rearrange("p (k c) -> p k c", k=8)), 0,
                          (lambda st, s=s: st[:, 0:8, s * 256:(s + 1) * 256])) for s in range(2)]
                off = int(seq_offs[B_WO + 2 * cg])
                unit([(0, rows_view(w_o, 0, 8, cg * 512, 512), 8, 512)], casts, wscr[:, off:off + 2 * WI])

            if pend[0] is not None:
                pend[0]()
            P.barrier()

        hT = sb("hT", [128, 8, TP], F32)
        xio = [sb(f"xio{i}", [128, D], F32) for i in range(2)]
        xnT = sb("xnT", [128, 8, TP], BF16)
        xin = [sb(f"xin{i}", [128, D], F32) for i in range(4)]
        R1 = sb("R1", [128, 24, TP], BF16)
        XA = sb("XA", [128, 8, TP + 4], BF16)
        xrh = sb("xrh", [128, 8, 4], BF16)
        qT = sb("qT", [128, 8, TP], BF16)
        kT = sb("kT", [128, 2, 128 + TP], BF16)
        vtok = sb("vtok", [128, 5, 256], BF16)
        recT = sb("recT", [128, 8, TP], BF16)
        attT = sb("attT", [128, 8, TP], BF16)
        tmps = [sb(f"tmp{i}", [128, TP], F32) for i in range(NTMP)]
        sqs = [sb(f"sq{i}", [128, TP], BF16) for i in range(3)]
        xcbs = [sb(f"xcb{i}", [128, TP], BF16) for i in range(2)]
        pTs = [sb(f"pT{i}", [128, TP], BF16) for i in range(8)]
        wring = [sb(f"wr{i}", [128, SLOT], BF16) for i in range(NRING)]
        cst = sb("cst", [128, 8, 4, 3], F32)
        hfin = sb("hfin", [128, 8, 4], F32)
        rtok = sb("rtok", [128, 8], F32)
        kst = sb("kst", [128, 256], F32)
        vst = sb("vst", [128, 256], F32)
        cstT = sb("cstT", [128, 128], F32)
        lstT = sb("lstT", [128, 128], F32)
        kcT = sb("kcT", [128, 4, 2, 128], BF16)
        vc = sb("vc", [128, 4, 256], BF16)
        vsb = sb("vsb", [128, 4, 256], BF16)
        PS = [es.enter_context(nc.psum_tensor(f"ps{i}", [128, 512], F32)) for i in range(8)]

        ctr = {'ps': 0, 'tmp': 0, 'sq': 0, 'xio': 0, 'xcb': 0, 'xin': 0}

        ps_reserved = set()

        def psum(reserve=False):
            while True:
                i = ctr['ps'] % 8
                ctr['ps'] += 1
                if i not in ps_reserved:
                    break
            if reserve:
                ps_reserved.add(i)
            return PS[i], ("ps", i)

        def ps_release(key):
            ps_reserved.discard(key[1])

        def tmp():
            i = ctr['tmp'] % NTMP
            ctr['tmp'] += 1
            return tmps[i], ("tmp", i)

        def sqbuf():
            i = ctr['sq'] % 3
            ctr['sq'] += 1
            return sqs[i], ("sq", i)

        def xcbbuf():
            i = ctr['xcb'] % 2
            ctr['xcb'] += 1
            return xcbs[i], ("xcb", i)

        def xioslot():
            i = ctr['xio'] % 2
            ctr['xio'] += 1
            return i

        P.op('pool', MSET(cst[:], 0.0), [], [("cst", dc) for dc in range(8)])
        P.op('pool', MSET(hfin[:], 0.0), [], [("hfin", dc) for dc in range(8)])
        for i in range(8):
            P.op('pool', MSET(pTs[i][:], 0.0), [], [("pT", i, 0), ("pT", i, 1)])

        ntiles_total = NSEQ * ntile_seq + 1
        total_slabs = ntiles_total * NSLAB
        wst = {'issued': 0, 'cur': 0}

        def w_issue_upto(n):
            while wst['issued'] < min(n, total_slabs):
                i = wst['issued']
                s = i % NRING
                li = i % NSLAB
                sz = seq_sizes[li]
                off = int(seq_offs[li])
                P.dma('sp', f'W{s}', DMA(wring[s][:, 0:sz], wscr[:, off:off + sz]), [], [("wslot", s)])
                wst['issued'] += 1

        def w_next():
            i = wst['cur']
            w_issue_upto(i + 1)
            s = i % NRING
            return wring[s], ("wslot", s)

        def w_get(k):
            i = wst['cur'] + k
            w_issue_upto(i + 1)
            sidx = i % NRING
            return wring[sidx], ("wslot", sidx)

        def w_done():
            wst['cur'] += 1
            w_issue_upto(wst['cur'] + NRING)

        w_issue_upto(NRING)

        class Stats:
            pass

        def stats_begin():
            st = Stats()
            st.ps, st.key = None, None
            st.n = 0
            st.pend = None
            return st

        def stats_add(st, T, dc):
            if st.ps is None:
                st.ps, st.key = psum(reserve=True)
            sq, sqk = sqbuf()
            if st.n % 2 == 0:
                P.op('act', ACT(sq[:, :T], hT[:, dc, :T], AF.Square), [("hT", dc)], [sqk])
            else:
                P.op('dve', TT(sq[:, :T], hT[:, dc, :T], hT[:, dc, :T], ALU.mult), [("hT", dc)], [sqk])
            n = st.n
            st.n += 1
            stats_flush(st)
            st.pend = (lambda: P.group('pe', [MM(st.ps[:, :T], ones_bf[:, :], sq[:, :T], n == 0, n == 7)], [sqk], [st.key]))

        def stats_flush(st):
            if getattr(st, 'pend', None) is not None:
                st.pend()
                st.pend = None

        def norm_finish(st, T, gcol):
            stats_flush(st)
            t1, t1k = tmp()
            rstd, rk = tmp()
            P.op('act', ACT(t1[:, :T], st.ps[:, :T], AF.Ln, bias=EPS, scale=1.0 / D), [st.key], [t1k])
            P.op('act', ACT(rstd[:, :T], t1[:, :T], AF.Exp, scale=-0.5), [t1k], [rk])
            ps_release(st.key)
            for dc in range(8):
                P.op('dve', STT(xnT[:, dc, :T], hT[:, dc, :T], vecs[:, gcol + dc:gcol + dc + 1], rstd[:, :T], ALU.mult, ALU.mult),
                     [("hT", dc), rk], [("xn", dc)])

        def preload_ln():
            P.op('act', ACT(dmy[:, 1:2], dmy[:, 0:1], AF.Ln), [("dmy",)], [("dmy", 1)])

        def norm_fm(T, gcol):
            st = stats_begin()
            for dc in range(8):
                stats_add(st, T, dc)
            norm_finish(st, T, gcol)

        def ffn(T, st=None):
            xnk = [("xn", k) for k in range(8)]
            sl = [w_get(0), w_get(1)]
            vv = [sl[i][0][:, 0:GU].rearrange("p (g k c) -> p g k c", g=2, k=8) for i in range(2)]
            pre = [[psum(), psum()] for _ in range(2)]
            for kc in range(8):
                for i in range(2):
                    for gu in range(2):
                        bank, bk = pre[i][gu]
                        P.group('pe', [MM(bank[:, :T], vv[i][:, gu, kc, :], xnT[:, kc, :T], kc == 0, kc == 7)], [sl[i][1], ("xn", kc)], [bk])
            for m in range(FC):
                if m < 2:
                    (pg, pgk), (pu, puk) = pre[m]
                else:
                    slot, wk = w_next()
                    v = slot[:, 0:GU].rearrange("p (g k c) -> p g k c", g=2, k=8)
                    pg, pgk = psum()
                    pu, puk = psum()
                    P.group('pe', [MM(pg[:, :T], v[:, 0, kc, :], xnT[:, kc, :T], kc == 0, kc == 7) for kc in range(8)], [wk] + xnk, [pgk])
                    P.group('pe', [MM(pu[:, :T], v[:, 1, kc, :], xnT[:, kc, :T], kc == 0, kc == 7) for kc in range(8)], [wk] + xnk, [puk])
                w_done()
                sg, sgk = tmp()
                P.op('act', ACT(sg[:, :T], pg[:, :T], AF.Silu), [pgk], [sgk])
                P.op('dve', TT(R1[:, m, :T], pu[:, :T], sg[:, :T], ALU.mult), [puk, sgk], [("R1", m)])
            preload_ln()
            for m in range(8):
                pd, pdk = psum()
                for half in range(2):
                    slot, wk = w_next()
                    v = slot[:, 0:DH].rearrange("p (k c) -> p k c", k=11)
                    P.group('pe', [MM(pd[:, :T], v[:, kk, :], R1[:, half * 11 + kk, :T], (half == 0 and kk == 0), (half == 1 and kk == 10))
                                   for kk in range(11)], [wk] + [("R1", half * 11 + kk) for kk in range(11)], [pdk])
                    w_done()
                P.op('dve', STT(hT[:, m, :T], pd[:, :T], 0.5, hT[:, m, :T], ALU.mult, ALU.add), [pdk, ("hT", m)], [("hT", m)])
                if st is not None:
                    stats_add(st, T, m)

        def issue_x(T, xsrc, row0):
            TB = min(T, 128)
            slots = []
            for tb in range(T // TB):
                i = ctr['xin'] % 4
                ctr['xin'] += 1
                P.dma('pool', f'XN{i}', DMA(xin[i][:TB, :], xsrc[row0 + tb * TB: row0 + (tb + 1) * TB, :]), [], [("xin", i)])
                slots.append(i)
            return slots

        def load_x(T, slots):
            TB = min(T, 128)
            ssp, ssk = psum(reserve=True)
            pend_mm = [None]
            for tb in range(T // TB):
                i = slots[tb]
                for half in range(2):
                    pt, ptk = psum()
                    fns = [TR(pt[:, j * TB:(j + 1) * TB], xin[i][:TB, (half * 4 + j) * 128:(half * 4 + j + 1) * 128], ident[:TB, :TB]) for j in range(4)]
                    P.group('pe', fns, [("xin", i)], [ptk])
                    hk = [("hT", half * 4 + j) for j in range(4)]
                    P.op('act', ACT(hT[:, half * 4:half * 4 + 4, tb * TB:(tb + 1) * TB], pt[:, 0:4 * TB].rearrange("p (j t) -> p j t", t=TB), AF.Copy),
                         [ptk], hk)
                    sq, sqk = sqbuf()
                    hv = hT[:, half * 4:half * 4 + 4, tb * TB:(tb + 1) * TB]
                    P.op('dve', TT(sq[:, 0:4 * TB].rearrange("p (j t) -> p j t", t=TB), hv, hv, ALU.mult), hk, [sqk])
                    if pend_mm[0] is not None:
                        pend_mm[0]()

                    def mm(sq=sq, sqk=sqk, tb=tb, half=half):
                        P.group('pe', [MM(ssp[:, tb * TB:(tb + 1) * TB], ones_bf[:, :], sq[:, j * TB:(j + 1) * TB], (half == 0 and j == 0), (half == 1 and j == 3))
                                       for j in range(4)], [sqk], [ssk])
                    pend_mm[0] = mm
            st = Stats()
            st.ps, st.key, st.n = ssp, ssk, 8
            st.pend = pend_mm[0]
            return st

        def final_out(T, ydst, row0):
            TB = min(T, 128)
            NBK = T // TB
            for dc in range(8):
                if dc % 2 == 0:
                    P.op('act', ACT(xnT[:, dc, :T], hT[:, dc, :T], AF.Square), [("hT", dc)], [("xn", dc)])
                else:
                    P.op('dve', TT(xnT[:, dc, :T], hT[:, dc, :T], hT[:, dc, :T], ALU.mult), [("hT", dc)], [("xn", dc)])
            pss, pssk = psum()
            for tb in range(NBK):
                P.group('pe', [MM(pss[:TB, tb:tb + 1], xnT[:, dc, tb * TB:(tb + 1) * TB], ones_bf[:, 0:1], dc == 0, dc == 7) for dc in range(8)],
                        [("xn", dc) for dc in range(8)], [pssk])
            P.op('act', ACT(rtok[:TB, 4:4 + NBK], pss[:TB, 0:NBK], AF.Ln, bias=EPS, scale=1.0 / D), [pssk], [("rtok", 1)])
            P.op('act', ACT(rtok[:TB, 0:NBK], rtok[:TB, 4:4 + NBK], AF.Exp, scale=-0.5), [("rtok", 1)], [("rtok", 0)])
            for tb in range(NBK):
                s = xioslot()
                for half in range(2):
                    pt, ptk = psum()
                    fns = [TR(pt[:TB, j * 128:(j + 1) * 128], hT[:, half * 4 + j, tb * TB:(tb + 1) * TB], ident[:, :]) for j in range(4)]
                    P.group('pe', fns, [("hT", half * 4 + j) for j in range(4)], [ptk])
                    P.op('dve', STT(xio[s][:TB, half * 512:(half + 1) * 512], pt[:TB, :], rtok[:TB, tb:tb + 1], gtile[:TB, half * 512:(half + 1) * 512],
                                    ALU.mult, ALU.mult), [ptk, ("rtok", 0)], [("xio", s)])
                P.dma('pool', f'XO{s}', DMA(ydst[row0 + tb * TB: row0 + (tb + 1) * TB, :], xio[s][:TB, :]), [("xio", s)], [])

        def small_out_T(src_ap, ncol, stage, stk, sem, dsts):
            pt, ptk = psum()
            P.group('pe', [TR(pt[:ncol, 0:128], src_ap, ident[:, :])], [stk + ("src",)], [ptk])
            P.op('act', ACT(stage[:ncol, :], pt[:ncol, 0:128], AF.Copy), [ptk], [stk])
            P.dma_batch('pool', sem, [(DMA(d, stage[r0:r0 + nr, :]), [stk], []) for d, r0, nr in dsts])

        def mix(T, kind, first, last, seqi, st2=None):
            xnk = [("xn", k) for k in range(8)]
            if kind == 'p':
                segs = [(0, T, 0)]
                LSEG = T
            else:
                segs = [(b * 16, 16, b * 20) for b in range(4)]
                LSEG = 16
            nseg = len(segs)
            need_state = last or kind == 's'

            def xa_new(dc):
                if kind == 'p':
                    return XA[:, dc, 4:4 + T]
                return XA[:, dc, 0:80].rearrange("p (s l) -> p s l", l=20)[:, :, 4:20]

            def ps_seg(p):
                if kind == 'p':
                    return p[:, :T]
                return p[:, 0:64].rearrange("p (s l) -> p s l", l=16)

            def lru_a(dc):
                pc, pck = psum()
                for (t0, L, off) in segs:
                    P.group('pe', [MM(pc[:, t0:t0 + L], Dg[:, j, dc, :], XA[:, dc, off + 1 + j:off + 1 + j + L], j == 0, j == 3) for j in range(4)],
                            [("XA", dc)], [pck])
                xc, xck = tmp()
                P.op('act', ACT(xc[:, :T], pc[:, :T], AF.Identity, bias=vecs[:, VC_CB + dc:VC_CB + dc + 1], scale=1.0), [pck], [xck])
                xcb, xcbk = xcbbuf()
                P.op('dve', CP(xcb[:, :T], xc[:, :T]), [xck], [xcbk])
                return (dc, xc, xck, xcb, xcbk)

            def lru_b(st):
                dc, xc, xck, xcb, xcbk = st
                pr, prk = psum()
                pi, pik = psum()
                P.group('pe', [MM(pr[:, :T], wrg[:, dc, :], xcb[:, :T], True, True)], [xcbk], [prk])
                P.group('pe', [MM(pi[:, :T], wig[:, dc, :], xcb[:, :T], True, True)], [xcbk], [pik])
                r, rk = tmp()
                ig, igk = tmp()
                P.op('act', ACT(r[:, :T], pr[:, :T], AF.Exp, bias=dv[:, 16 + dc:17 + dc], scale=-1.0), [prk], [rk])
                P.op('act', ACT(ig[:, :T], pi[:, :T], AF.Exp, bias=dv[:, 24 + dc:25 + dc], scale=-1.0), [pik], [igk])
                P.op('act', ACT(r[:, :T], r[:, :T], AF.Ln, bias=1.0, scale=1.0), [rk], [rk])
                P.op('act', ACT(ig[:, :T], ig[:, :T], AF.Ln, bias=1.0, scale=1.0), [igk], [igk])
                P.op('act', ACT(r[:, :T], r[:, :T], AF.Exp, scale=-1.0), [rk], [rk])
                P.op('act', ACT(ig[:, :T], ig[:, :T], AF.Exp, scale=-1.0), [igk], [igk])
                a, ak = tmp()
                e2, e2k = tmp()
                P.op('act', ACT(a[:, :T], r[:, :T], AF.Exp, scale=dv[:, dc:dc + 1]), [rk], [ak])
                P.op('act', ACT(e2[:, :T], r[:, :T], AF.Exp, scale=dv[:, 8 + dc:9 + dc]), [rk], [e2k])
                P.op('dve', TT(ig[:, :T], ig[:, :T], xc[:, :T], ALU.mult), [igk, xck], [igk])
                P.op('act', ACT(e2[:, :T], e2[:, :T], AF.Ln, bias=1.0, scale=-1.0), [e2k], [e2k])
                P.op('act', ACT(e2[:, :T], e2[:, :T], AF.Exp, scale=0.5), [e2k], [e2k])
                P.op('dve', TT(ig[:, :T], ig[:, :T], e2[:, :T], ALU.mult), [igk, e2k], [igk])
                hl, hlk = tmp()
                for si, (t0, L, off) in enumerate(segs):
                    if kind == 'p':
                        init = 0.0 if first else hcar[:, dc:dc + 1]
                        rd = [ak, igk] + ([] if first else [("hcar", dc)])
                    else:
                        init = hst[:, dc, si:si + 1]
                        rd = [ak, igk]
                    P.op('dve', SCAN(hl[:, t0:t0 + L], a[:, t0:t0 + L], ig[:, t0:t0 + L], init), rd, [hlk])
                if kind == 'p':
                    P.op('act', ACT(hcar[:, dc:dc + 1], hl[:, T - 1:T], AF.Copy), [hlk], [("hcar", dc)])
                    if last:
                        P.op('act', ACT(hfin[:, dc, 0:1], hl[:, T - 1:T], AF.Copy), [hlk], [("hfin", dc)])
                else:
                    P.op('act', ACT(hfin[:, dc, :], hl[:, 0:64].rearrange("p (s l) -> p s l", l=16)[:, :, 15], AF.Copy), [hlk], [("hfin", dc)])
                P.op('dve', CP(recT[:, dc, :T], hl[:, :T]), [hlk], [("rec", dc)])

            def lru_chunk(dc):
                lru_b(lru_a(dc))

            def v3(ap, g=4):
                return ap.rearrange("p (g q) -> p g q", g=g)
            def attn_A(i):
                qb, Pp = i // 2, i % 2
                buf = i % 2
                blks = []
                if not (first and qb == 0):
                    blks.append((0, qb * 128, qb))
                blks.append((1, (qb + 1) * 128, qb + 1))
                qk = [("q", Pp * 4 + g) for g in range(4)]
                for half in range(2):
                    rows = slice(half * 64, half * 64 + 64)
                    for (role, kc0, vb) in blks:
                        ps_, psk = psum()
                        P.group('pe', [MM(ps_[:, :], kT[rows, Pp, kc0:kc0 + 128], qT[rows, Pp * 4:Pp * 4 + 4, qb * 128:(qb + 1) * 128], True, True)],
                                [("kT", Pp)] + qk, [psk])
                        pi_ = buf * 4 + half * 2 + role
                        pt = pTs[pi_]
                        if role == 0:
                            P.op('act', ACT(v3(pt[0:64, :])[:, :, 0:64], v3(ps_[0:64, :])[:, :, 0:64], AF.Exp, scale=0.125), [psk], [("pT", pi_, 0)])
                            P.op('act', ACT(pt[64:128, :], ps_[64:128, :], AF.Exp, scale=0.125), [psk], [("pT", pi_, 1)])
                        else:
                            P.op('act', ACT(pt[0:64, :], ps_[0:64, :], AF.Exp, scale=0.125), [psk], [("pT", pi_, 0)])
                            P.op('act', ACT(v3(pt[64:128, :])[:, :, 64:128], v3(ps_[64:128, :])[:, :, 64:128], AF.Exp, scale=0.125), [psk], [("pT", pi_, 1)])
                return blks

            def attn_B(i, blks):
                qb, Pp = i // 2, i % 2
                buf = i % 2
                pa, pak = psum()
                pb, pbk = psum()
                for half in range(2):
                    rows = slice(half * 64, half * 64 + 64)
                    kvh = 2 * Pp + half
                    for bi, (role, kc0, vb) in enumerate(blks):
                        pi_ = buf * 4 + half * 2 + role
                        pt = pTs[pi_]
                        ptk = [("pT", pi_, 0), ("pT", pi_, 1)]
                        P.group('pe', [MM(pa[rows, :], vtok[:, vb, kvh * 64:(kvh + 1) * 64], pt[:, :], bi == 0, bi == len(blks) - 1)],
                                [("vt", vb)] + ptk, [pak])
                        P.group('pe', [MM(pb[rows, :], ones_bf[:, 0:64], pt[:, :], bi == 0, bi == len(blks) - 1)], ptk, [pbk])
                den, dk = tmp()
                P.op('dve', TT(den[:, :], pb[:, :], sinkt[:, Pp, :], ALU.add), [pbk], [dk])
                P.op('act', ACT(den[:, :], den[:, :], AF.Ln), [dk], [dk])
                P.op('act', ACT(den[:, :], den[:, :], AF.Exp, scale=-1.0), [dk], [dk])
                P.op('dve', TT(attT[:, Pp * 4:Pp * 4 + 4, qb * 128:(qb + 1) * 128], v3(pa[:, :]), v3(den[:, :]), ALU.mult), [pak, dk],
                     [("att", Pp * 4 + g) for g in range(4)])


            attn_state = {'i': 0, 'prev': None}

            def attn_step():
                st = attn_state
                new = None
                if st['i'] < 8:
                    blks = attn_A(st['i'])
                    new = (st['i'], blks)
                    st['i'] += 1
                if st['prev'] is not None:
                    attn_B(*st['prev'])
                st['prev'] = new

            lru_pend = [None]
            lru_next = [0]
            slw = [w_get(0), w_get(1)]
            vw = [slw[i][0][:, 0:WI].rearrange("p (k c) -> p k c", k=8) for i in range(2)]
            prew = {c: psum() for c in range(4)}
            for kc in range(8):
                for c in range(4):
                    bank, bk = prew[c]
                    P.group('pe', [MM(bank[:, :T], vw[c // 2][:, kc, (c % 2) * 128:(c % 2 + 1) * 128], xnT[:, kc, :T], kc == 0, kc == 7)],
                            [slw[c // 2][1], ("xn", kc)], [bk])

            def lru_step():
                st_new = None
                if lru_next[0] < 8:
                    st_new = lru_a(lru_next[0])
                    lru_next[0] += 1
                if lru_pend[0] is not None:
                    lru_b(lru_pend[0])
                lru_pend[0] = st_new

            for s in range(22):
                stage(3.06 + s * 0.001)
                if 2 <= s <= 10:
                    lru_step()
                if kind == 'p' and s >= 10:
                    attn_step()
                slot, wk = w_next()
                v = slot[:, 0:WI].rearrange("p (k c) -> p k c", k=8)
                if s == 9:
                    if kind == 'p':
                        for tb in range(4):
                            pv, pvk = psum()
                            P.group('pe', [MM(pv[:, 0:256], xnT[:, kc, tb * 128:(tb + 1) * 128], v[:, kc, :], kc == 0, kc == 7)
                                           for kc in range(8)], [wk] + xnk, [pvk])
                            P.op('act', ACT(vtok[:, 1 + tb, :], pv[:, 0:256], AF.Copy), [pvk], [("vt", 1 + tb)])
                            if last and tb == 3 and os.environ.get('KV', '1') == '1':
                                pvo, pvok = psum()
                                P.group('pe', [MM(pvo[:, 0:256], xnT[:, kc, T - 128:T], v[:, kc, :], kc == 0, kc == 7) for kc in range(8)], [wk] + xnk, [pvok])
                                P.op('dve', CP(vst[:, :], pvo[:, 0:256]), [pvok], [("vst",)])
                                P.dma('pool', 'OV', DMA(vpo[seqi, :, :], vst[:, :]), [("vst",)], [])
                    else:
                        pv, pvk = psum()
                        for b in range(2):
                            P.group('pe', [MM(pv[0:16, b * 256:(b + 1) * 256], xnT[:, kc, b * 16:(b + 1) * 16], v[:, kc, :], kc == 0, kc == 7)
                                           for kc in range(8)], [wk] + xnk, [pvk])
                        pv2, pv2k = psum()
                        for b in range(2):
                            P.group('pe', [MM(pv2[0:16, b * 256:(b + 1) * 256], xnT[:, kc, (b + 2) * 16:(b + 3) * 16], v[:, kc, :], kc == 0, kc == 7)
                                           for kc in range(8)], [wk] + xnk, [pv2k])
                        P.op('act', ACT(vsb[0:16, 0:2, :], pv[0:16, :].rearrange("p (j c) -> p j c", j=2), AF.Copy), [pvk], [("vsb", 0)])
                        P.op('act', ACT(vsb[0:16, 2:4, :], pv2[0:16, :].rearrange("p (j c) -> p j c", j=2), AF.Copy), [pv2k], [("vsb", 1)])
                        pv3, pv3k = psum()
                        P.group('pe', [MM(pv3[0:64, 0:256], xnT[:, kc, 0:64], v[:, kc, :], kc == 0, kc == 7) for kc in range(8)], [wk] + xnk, [pv3k])
                        P.op('dve', CP(vst[0:64, :], pv3[0:64, 0:256]), [pv3k], [("vst",)])
                        P.dma('pool', 'OV', DMA(vso[:, :], vst[0:64, :]), [("vst",)], [])
                    w_done()
                    continue
                for j in range(2):
                    c = 2 * s + j
                    if c in prew:
                        p, pk = prew[c]
                    else:
                        p, pk = psum()
                        P.group('pe', [MM(p[:, :T], v[:, kc, j * 128:(j + 1) * 128], xnT[:, kc, :T], kc == 0, kc == 7) for kc in range(8)], [wk] + xnk, [pk])
                    if c < 8:
                        dc = c
                        if kind == 'p':
                            if first:
                                P.op('pool', MSET(XA[:, dc, 0:4], 0.0), [], [("XA", dc)])
                            else:
                                P.op('pool', CP(XA[:, dc, 0:4], xrh[:, dc, :]), [("xrh", dc)], [("XA", dc)])
                        else:
                            P.op('pool', CP(XA[:, dc, 0:80].rearrange("p (s l) -> p s l", l=20)[:, :, 0:4], scv[:, dc, :, :]), [], [("XA", dc)])
                        P.op('act', ACT(xa_new(dc), ps_seg(p), AF.Copy), [pk], [("XA", dc)])
                        if kind == 'p' and not last:
                            P.op('pool', CP(xrh[:, dc, :], XA[:, dc, T:T + 4]), [("XA", dc)], [("xrh", dc)])
                        if need_state:
                            if kind == 'p':
                                P.op('dve', CP(cst[:, dc, 0, :], p[:, T - 3:T]), [pk, ("XA", dc)], [("cst", dc)])
                            else:
                                P.op('dve', CP(cst[:, dc, :, :], p[:, 0:64].rearrange("p (s l) -> p s l", l=16)[:, :, 13:16]), [pk, ("XA", dc)], [("cst", dc)])
                    elif c >= 36:
                        dc = c - 36
                        P.op('act', ACT(R1[:, dc, :T], p[:, :T], AF.Gelu_apprx_tanh), [pk], [("R1", dc)])
                        P.op('dve', TT(recT[:, dc, :T], R1[:, dc, :T], recT[:, dc, :T], ALU.mult), [("R1", dc), ("rec", dc)], [("rec", dc)])
                    elif c < 16:
                        jq = c - 8
                        P.op('dve', CP(qT[:, jq, :T], p[:, :T]), [pk], [("q", jq)])
                    elif c < 18:
                        kc_ = c - 16
                        if kind == 'p':
                            P.op('dve', CP(kT[:, kc_, 128:128 + T], p[:, :T]), [pk], [("kT", kc_)])
                        else:
                            P.op('dve', CP(kT[:, kc_, 0:T], p[:, :T]), [pk], [("kT", kc_)])
                    elif c < 28:
                        dc = c - 20
                        P.op('dve', CP(R1[:, 8 + dc, :T], p[:, :T]), [pk], [("R1", 8 + dc)])
                    else:
                        dc = c - 28
                        P.op('dve', CP(R1[:, 16 + dc, :T], p[:, :T]), [pk], [("R1", 16 + dc)])
                if s == 8 and need_state:
                    pkk, pkkk = psum()
                    if kind == 'p':
                        P.group('pe', [MM(pkk[:, 0:256], xnT[:, kc, T - 128:T], v[:, kc, :], kc == 0, kc == 7) for kc in range(8)], [wk] + xnk, [pkkk])
                        P.op('dve', CP(kst[:, :], pkk[:, 0:256]), [pkkk], [("kst",)])
                        P.dma('pool', 'OK', DMA(kpo[seqi, :, :], kst[:, :]), [("kst",)], [])
                    else:
                        P.group('pe', [MM(pkk[0:64, 0:256], xnT[:, kc, 0:64], v[:, kc, :], kc == 0, kc == 7) for kc in range(8)], [wk] + xnk, [pkkk])
                        P.op('dve', CP(kst[0:64, :], pkk[0:64, 0:256]), [pkkk], [("kst",)])
                        P.dma('pool', 'OK', DMA(kso[:, :], kst[0:64, :]), [("kst",)], [])
                w_done()

            while lru_next[0] < 8 or lru_pend[0] is not None:
                lru_step()
            stage(3.3)
            it = 0
            if kind == 'p':
                while attn_state['i'] < 8 or attn_state['prev'] is not None:
                    attn_step()
                if not last:
                    P.op('pool', CP(kT[:, :, 0:128], kT[:, :, T:T + 128]), [("kT", 0), ("kT", 1)], [("kT", 0), ("kT", 1)])
                    P.op('pool', CP(vtok[:, 0, :], vtok[:, 4, :]), [("vt", 4)], [("vt", 0)])
            else:
                s0 = xioslot()
                P.dma('pool', f'XI{s0}', DMA(xio[s0][:, :].rearrange("k (b f) -> k b f", b=4), ckd.rearrange("b k f -> k b f")), [], [("xio", s0)])
                s1 = xioslot()
                P.dma('pool', f'XI{s1}', DMA(xio[s1][:, :].rearrange("k (b f) -> k b f", b=4), cvd.rearrange("b k f -> k b f")), [], [("xio", s1)])
                for b in range(4):
                    pt, ptk = psum()
                    P.group('pe', [TR(pt[:, c * 128:(c + 1) * 128], xio[s0][:, b * 256 + c * 128: b * 256 + (c + 1) * 128], ident[:, :]) for c in range(2)],
                            [("xio", s0)], [ptk])
                    P.op('dve', CP(kcT[:, b, :, :], pt[:, 0:256].rearrange("p (c k) -> p c k", c=2)), [ptk], [("kcT", b)])
                P.op('act', ACT(vc[:, :, :].rearrange("p b f -> p (b f)"), xio[s1][:, :], AF.Copy), [("xio", s1)], [("vc",)])
                for b in range(4):
                    for Pp in range(2):
                        buf = it % 2
                        it += 1
                        pa, pak = psum()
                        pb, pbk = psum()
                        qk = [("q", Pp * 4 + g) for g in range(4)]
                        for half in range(2):
                            rows = slice(half * 64, half * 64 + 64)
                            kvh = 2 * Pp + half
                            rhs = qT[rows, Pp * 4:Pp * 4 + 4, b * 16:(b + 1) * 16]
                            ps1, ps1k = psum()
                            P.group('pe', [MM(ps1[:, 0:64], kcT[rows, b, Pp, :], rhs, True, True)], [("kcT", b)] + qk, [ps1k])
                            ps2, ps2k = psum()
                            P.group('pe', [MM(ps2[0:16, 0:64], kT[rows, Pp, b * 16:(b + 1) * 16], rhs, True, True)], [("kT", Pp)] + qk, [ps2k])
                            i1 = buf * 4 + half * 2
                            i2 = i1 + 1
                            P.op('act', ACT(pTs[i1][:, 0:64], ps1[:, 0:64], AF.Exp, scale=0.125), [ps1k], [("pT", i1, 0), ("pT", i1, 1)])
                            P.op('act', ACT(pTs[i2][0:16, 0:64], ps2[0:16, 0:64], AF.Exp, scale=0.125), [ps2k], [("pT", i2, 0), ("pT", i2, 1)])
                            k1 = [("pT", i1, 0), ("pT", i1, 1)]
                            k2 = [("pT", i2, 0), ("pT", i2, 1)]
                            P.group('pe', [MM(pa[rows, 0:64], vc[:, b, kvh * 64:(kvh + 1) * 64], pTs[i1][:, 0:64], True, False)], [("vc",)] + k1, [pak])
                            P.group('pe', [MM(pa[rows, 0:64], vsb[0:16, b, kvh * 64:(kvh + 1) * 64], pTs[i2][0:16, 0:64], False, True)],
                                    [("vsb", b // 2)] + k2, [pak])
                            P.group('pe', [MM(pb[rows, 0:64], ones_bf[:, 0:64], pTs[i1][:, 0:64], True, False)], k1, [pbk])
                            P.group('pe', [MM(pb[rows, 0:64], ones_bf[0:16, 0:64], pTs[i2][0:16, 0:64], False, True)], k2, [pbk])
                        den, dk = tmp()
                        P.op('dve', TT(v3(den[:, 0:64]), v3(pb[:, 0:64]), v3(sinkt[:, Pp, :])[:, :, 0:16], ALU.add), [pbk], [dk])
                        P.op('dve', RECIP(den[:, 0:64], den[:, 0:64]), [dk], [dk])
                        P.op('dve', TT(attT[:, Pp * 4:Pp * 4 + 4, b * 16:(b + 1) * 16], v3(pa[:, 0:64]), v3(den[:, 0:64]), ALU.mult), [pak, dk],
                             [("att", Pp * 4 + g) for g in range(4)])

            stage(3.2)
            if need_state:
                ncs = 12
                nsq = 4
                cdst = cpo if kind == 'p' else cso
                ldst = lpo if kind == 'p' else lso
                pt, ptk = psum()
                P.group('pe', [TR(pt[:96, 0:128], cst[:, :, :, :].rearrange("p a b c -> p (a b c)"), ident[:, :])],
                        [("cst", dc) for dc in range(8)], [ptk])
                P.op('act', ACT(cstT[:8 * ncs, :], pt[:8 * ncs, 0:128], AF.Copy), [ptk], [("cstT",)])
                items = []
                for dc in range(8):
                    if kind == 'p':
                        d = cdst[seqi, :, dc * 128:(dc + 1) * 128]
                    else:
                        d = cdst[:, :, dc * 128:(dc + 1) * 128].rearrange("s r p -> (s r) p")
                    items.append((DMA(d, cstT[dc * 12:dc * 12 + 3 * nseg, :]), [("cstT",)], []))
                P.dma_batch('pool', 'OC', items)
                pt2, pt2k = psum()
                P.group('pe', [TR(pt2[:32, 0:128], hfin[:, :, :].rearrange("p a b -> p (a b)"), ident[:, :])],
                        [("hfin", dc) for dc in range(8)], [pt2k])
                P.op('act', ACT(lstT[:32, :], pt2[:32, 0:128], AF.Copy), [pt2k], [("lstT",)])
                items = []
                for dc in range(8):
                    if kind == 'p':
                        d = ldst[seqi:seqi + 1, dc * 128:(dc + 1) * 128]
                    else:
                        d = ldst[:, dc * 128:(dc + 1) * 128]
                    items.append((DMA(d, lstT[dc * 4:dc * 4 + nseg, :]), [("lstT",)], []))
                P.dma_batch('pool', 'OL', items)

            stage(3.4)
            for m in range(8):
                P.op('act', ACT(R1[:, 8 + m, :T], R1[:, 8 + m, :T], AF.Tanh, scale=0.5), [("R1", 8 + m)], [("R1", 8 + m)])
                P.op('act', ACT(R1[:, 16 + m, :T], R1[:, 16 + m, :T], AF.Tanh, scale=0.5), [("R1", 16 + m)], [("R1", 16 + m)])
            reck = [("rec", k) for k in range(8)]
            attk = [("att", k) for k in range(8)]
            for sidx in range(4):
                pbr = []
                slot, wk = w_next()
                v = slot[:, 0:WI].rearrange("p (k c) -> p k c", k=8)
                for j in range(2):
                    p, pk = psum()
                    P.group('pe', [MM(p[:, :T], v[:, kc, j * 128:(j + 1) * 128], recT[:, kc, :T], kc == 0, kc == 7) for kc in range(8)], [wk] + reck, [pk])
                    pbr.append((p, pk))
                w_done()
                slot, wk = w_next()
                v = slot[:, 0:WI].rearrange("p (k c) -> p k c", k=8)
                pba = []
                for j in range(2):
                    p, pk = psum()
                    P.group('pe', [MM(p[:, :T], v[:, kc, j * 128:(j + 1) * 128], attT[:, kc, :T], kc == 0, kc == 7) for kc in range(8)], [wk] + attk, [pk])
                    pba.append((p, pk))
                w_done()
                for j in range(2):
                    m = sidx * 2 + j
                    m1, m1k = tmp()
                    m2, m2k = tmp()
                    P.op('dve', STT(m1[:, :T], R1[:, 8 + m, :T], 1.0, pbr[j][0][:, :T], ALU.add, ALU.mult), [("R1", 8 + m), pbr[j][1]], [m1k])
                    P.op('dve', STT(m2[:, :T], R1[:, 16 + m, :T], 1.0, pba[j][0][:, :T], ALU.add, ALU.mult), [("R1", 16 + m), pba[j][1]], [m2k])
                    P.op('dve', TT(qT[:, m, :T], m1[:, :T], m2[:, :T], ALU.add), [m1k, m2k], [("q", m)])
            preload_ln()
            mk = [("q", k) for k in range(8)]
            for sidx in range(4):
                slot, wk = w_next()
                v = slot[:, 0:WI].rearrange("p (k c) -> p k c", k=8)
                for j in range(2):
                    m = sidx * 2 + j
                    po, pok = psum()
                    P.group('pe', [MM(po[:, :T], v[:, kc, j * 128:(j + 1) * 128], qT[:, kc, :T], kc == 0, kc == 7) for kc in range(8)], [wk] + mk, [pok])
                    P.op('dve', STT(hT[:, m, :T], po[:, :T], 0.5, hT[:, m, :T], ALU.mult, ALU.add), [pok, ("hT", m)], [("hT", m)])
                    if st2 is not None:
                        stats_add(st2, T, m)
                w_done()

        import os
        KSTOP = float(os.environ.get("KSTOP", "99"))

        class _Stop(Exception):
            pass

        def stage(n):
            if KSTOP <= n:
                raise _Stop()

        tiles = []
        for seqi in range(NSEQ):
            for t in range(ntile_seq):
                tiles.append(('p', TP, xp, yp, seqi * LP + t * TP, t == 0, t == ntile_seq - 1, seqi))
        tiles.append(('s', TS, xs, ys, 0, True, True, 0))

        def tile(idx, slots):
            kind, T, xsrc, ydst, row0, first, last, seqi = tiles[idx]
            stage(0)
            st1 = load_x(T, slots)
            stage(1)
            norm_finish(st1, T, VC_G1)
            stage(2)
            stm = stats_begin()
            ffn(T, stm)
            stage(3)
            norm_finish(stm, T, VC_GM)
            stage(3.05)
            st2 = stats_begin()
            mix(T, kind, first, last, seqi, st2)
            stage(4)
            norm_finish(st2, T, VC_G2)
            nxt = None
            if idx + 1 < len(tiles):
                nk, nT, nx, ny, nr0 = tiles[idx + 1][:5]
                nxt = issue_x(nT, nx, nr0)
            ffn(T)
            final_out(T, ydst, row0)
            stage(5)
            return nxt

        try:
            slots = issue_x(tiles[0][1], tiles[0][2], tiles[0][4])
            for idx in range(len(tiles)):
                slots = tile(idx, slots)
        except _Stop:
            pass

        P.final_wait('sp')
        P.final_wait('pool')

        for name in sorted(P.semnames):
            P.sems[name] = es.enter_context(nc.semaphore(name))
        block = es.enter_context(nc.Block())

        @block.tensor
        def _(h):
            for f in P.prog['pe']:
                f(h)

        @block.scalar
        def _(h):
            for f in P.prog['act']:
                f(h)

        @block.vector
        def _(h):
            for f in P.prog['dve']:
                f(h)

        @block.gpsimd
        def _(h):
            for f in P.prog['pool']:
                f(h)

        @block.sync
        def _(h):
            for f in P.prog['sp']:
                f(h)
    return nc


def _qperm():
    idx = []
    for Pp in range(2):
        for g in range(4):
            for half in range(2):
                head = (2 * Pp + half) * 4 + g
                idx.extend(range(head * 64, head * 64 + 64))
    return np.array(idx, dtype=np.int64)


def _prep_shared(inp):
    f = lambda a: np.ascontiguousarray(np.asarray(a, dtype=np.float32))
    qp = _qperm()
    w_in = np.asarray(inp['w_in'][0], dtype=np.float32)
    cols = np.concatenate([np.arange(0, 1024), 2048 + qp, np.arange(3072, 3328), np.arange(3328, 3584),
                           np.arange(3584, 4608), np.arange(4608, 5632), np.arange(1024, 2048)])
    w_in_p = f(w_in[:, cols])
    wb = np.asarray(inp['w_branch'][0], dtype=np.float32)
    rows = np.arange(2048)
    rows[1024:2048] = 1024 + qp
    wb_p = f(wb[rows, :])

    def fm(vec):
        return np.asarray(vec, dtype=np.float32).reshape(8, 128).T
    vecs = np.zeros((128, NVEC), np.float32)
    vecs[:, VC_G1:VC_G1 + 8] = fm(inp['norm_ff1'][0])
    vecs[:, VC_GM:VC_GM + 8] = fm(inp['norm_mix'][0])
    vecs[:, VC_G2:VC_G2 + 8] = fm(inp['norm_ff2'][0])
    vecs[:, VC_CB:VC_CB + 8] = fm(inp['conv_b'][0])
    vecs[:, VC_BRG:VC_BRG + 8] = np.asarray(inp['b_rg'][0], np.float32).T
    vecs[:, VC_BIG:VC_BIG + 8] = np.asarray(inp['b_ig'][0], np.float32).T
    vecs[:, VC_LAM:VC_LAM + 8] = fm(inp['lru_lambda'][0])
    cw = np.asarray(inp['conv_w'][0], np.float32)
    for j in range(4):
        vecs[:, VC_CW + j * 8:VC_CW + (j + 1) * 8] = fm(cw[j])
    wrg = f(np.asarray(inp['w_rg'][0], np.float32).transpose(1, 0, 2).reshape(128, 1024))
    wig = f(np.asarray(inp['w_ig'][0], np.float32).transpose(1, 0, 2).reshape(128, 1024))
    return {
        'vecs': vecs, 'gf': f(inp['norm_final']), 'sinks': f(inp['attn_sinks'][0]),
        'wrg': wrg, 'wig': wig,
        'ff1_gate': f(inp['ff1_gate'][0]), 'ff1_up': f(inp['ff1_up'][0]), 'ff1_down': f(inp['ff1_down'][0]),
        'ff2_gate': f(inp['ff2_gate'][0]), 'ff2_up': f(inp['ff2_up'][0]), 'ff2_down': f(inp['ff2_down'][0]),
        'w_in': w_in_p, 'w_branch': wb_p, 'w_out': f(inp['w_out'][0]),
    }


def run_step(inp, seq_len):
    ntile_seq = seq_len // TP
    xpf = np.asarray(inp['x_prompt'], np.float32)
    xsf = np.asarray(inp['x_sample'], np.float32)
    B = xpf.shape[0]
    assert B == 2 * NCORES and xsf.shape[0] == 4 * NCORES
    shared = _prep_shared(inp)
    ck = np.asarray(inp['cache_k'][0], np.float32).reshape(32, 128, 256)
    cv = np.asarray(inp['cache_v'][0], np.float32).reshape(32, 128, 256)
    sconv = np.asarray(inp['state_conv'][0], np.float32)
    slru = np.asarray(inp['state_lru'][0], np.float32)
    in_maps = []
    for c in range(NCORES):
        m = dict(shared)
        m['xp'] = np.ascontiguousarray(xpf[2 * c:2 * c + 2, :seq_len].reshape(2 * seq_len, D))
        m['xs'] = np.ascontiguousarray(xsf[4 * c:4 * c + 4].reshape(TS, D))
        m['ck'] = np.ascontiguousarray(ck[4 * c:4 * c + 4])
        m['cv'] = np.ascontiguousarray(cv[4 * c:4 * c + 4])
        sc = sconv[4 * c:4 * c + 4]
        scp = np.zeros((128, 8, 4, 4), np.float32)
        scp[:, :, :, 1:4] = sc.reshape(4, 3, 8, 128).transpose(3, 2, 0, 1)
        m['scv'] = np.ascontiguousarray(scp.reshape(128, 128))
        sl = slru[4 * c:4 * c + 4]
        m['hst'] = np.ascontiguousarray(sl.reshape(4, 8, 128).transpose(2, 1, 0).reshape(128, 32))
        in_maps.append(m)
    nc = build_program(ntile_seq)
    res = run_bass_kernel_spmd(nc, in_maps, core_ids=list(range(NCORES)))
    R = res.results
    y_prompt = np.stack([R[c]['yp'].reshape(2, seq_len, D) for c in range(NCORES)]).reshape(B, seq_len, D)
    y_sample = np.stack([R[c]['ys'].reshape(4, 16, D) for c in range(NCORES)]).reshape(32, 16, D)
    kp = np.concatenate([R[c]['kp'] for c in range(NCORES)]).reshape(1, B, 128, 4, 64)
    vp = np.concatenate([R[c]['vp'] for c in range(NCORES)]).reshape(1, B, 128, 4, 64)
    cp = np.concatenate([R[c]['cp'] for c in range(NCORES)]).reshape(1, B, 3, D)
    lp = np.concatenate([R[c]['lp'] for c in range(NCORES)]).reshape(1, B, D)
    ks = np.concatenate([R[c]['ks'].reshape(4, 16, 256) for c in range(NCORES)]).reshape(1, 32, 16, 4, 64)
    vs = np.concatenate([R[c]['vs'].reshape(4, 16, 256) for c in range(NCORES)]).reshape(1, 32, 16, 4, 64)
    cs = np.concatenate([R[c]['cs'] for c in range(NCORES)]).reshape(1, 32, 3, D)
    ls = np.concatenate([R[c]['ls'] for c in range(NCORES)]).reshape(1, 32, D)
    outs = (y_prompt, y_sample, kp, vp, cp, lp, ks, vs, cs, ls)
    return tuple(np.ascontiguousarray(o.astype(np.float32)) for o in outs)


def kernel(**inputs):
    return run_step(inputs, 4096)
```
